# Optimizing a Trainium2 kernel written in Bass

```python
import math
import jax, jax.numpy as jnp
from jax import lax
import numpy as np

D_MODEL = 1024
BATCH = 8
SEQ = 2048
DEPTH = 1

ROPE_THETA = 10000.0
NORM_EPS = 1e-6
NEG_INF = -1e30

MLA_HEADS = 4
MLA_NOPE_DIM = 128
MLA_ROPE_DIM = 64
MLA_V_DIM = 128
MLA_Q_RANK = 256
MLA_KV_RANK = 128
MLA_QK_DIM = MLA_NOPE_DIM + MLA_ROPE_DIM
MLA_WIDTH = MLA_HEADS * MLA_V_DIM
MLA_QUERY_BLOCK = 128

SWA_Q_HEADS = 8
SWA_KV_HEADS = 2
SWA_HEAD_DIM = 64
SWA_WINDOW = 128
SWA_BLOCK = 128
SWA_WIDTH = SWA_Q_HEADS * SWA_HEAD_DIM

MIX_WIDTH = MLA_WIDTH + SWA_WIDTH

IN_SIZES = (
    MLA_Q_RANK,
    MLA_KV_RANK,
    MLA_ROPE_DIM,
    MLA_WIDTH,
    SWA_Q_HEADS * SWA_HEAD_DIM,
    SWA_KV_HEADS * SWA_HEAD_DIM,
    SWA_KV_HEADS * SWA_HEAD_DIM,
    SWA_WIDTH,
)
IN_WIDTH = sum(IN_SIZES)
IN_SPLITS = tuple(int(v) for v in np.cumsum(IN_SIZES)[:-1])

kernel_name = "hymba_mla_swa_sink_gated_block"


def rms_norm(x, g):
    xf = x.astype(jnp.float32)
    y = xf * lax.rsqrt(jnp.mean(xf * xf, axis=-1, keepdims=True) + NORM_EPS)
    return (y * g.astype(jnp.float32)).astype(x.dtype)


def rope_tables(seq, dim):
    pos = jnp.arange(seq, dtype=jnp.float32)
    inv_freq = 1.0 / (ROPE_THETA ** (jnp.arange(0, dim, 2, dtype=jnp.float32) / dim))
    ang = pos[:, None] * inv_freq[None, :]
    return jnp.cos(ang), jnp.sin(ang)


def apply_rope(x, cos, sin):
    half = x.shape[-1] // 2
    shape = (cos.shape[0],) + (1,) * (x.ndim - 3) + (half,)
    c = cos.reshape(shape).astype(x.dtype)
    s = sin.reshape(shape).astype(x.dtype)
    x1, x2 = x[..., :half], x[..., half:]
    return jnp.concatenate([x1 * c - x2 * s, x2 * c + x1 * s], axis=-1)


def mla_attention(q_nope, q_rope, k_nope, k_rope, v):
    B, S, H, _ = q_nope.shape
    nb = S // MLA_QUERY_BLOCK
    scale = 1.0 / math.sqrt(MLA_QK_DIM)
    kpos = jnp.arange(S)

    def one_block(args):
        qn, qr, i = args
        s = (jnp.einsum('bqhd,bkhd->bhqk', qn, k_nope).astype(jnp.float32)
             + jnp.einsum('bqhr,bkr->bhqk', qr, k_rope).astype(jnp.float32)) * scale
        qpos = i * MLA_QUERY_BLOCK + jnp.arange(MLA_QUERY_BLOCK)
        causal = kpos[None, :] <= qpos[:, None]
        s = jnp.where(causal[None, None], s, NEG_INF)
        p = jax.nn.softmax(s, axis=-1).astype(v.dtype)
        return jnp.einsum('bhqk,bkhd->bqhd', p, v)

    qn_b = q_nope.reshape(B, nb, MLA_QUERY_BLOCK, H, MLA_NOPE_DIM).transpose(1, 0, 2, 3, 4)
    qr_b = q_rope.reshape(B, nb, MLA_QUERY_BLOCK, H, MLA_ROPE_DIM).transpose(1, 0, 2, 3, 4)
    out = lax.map(one_block, (qn_b, qr_b, jnp.arange(nb)))
    return out.transpose(1, 0, 2, 3, 4).reshape(B, S, H * MLA_V_DIM)


def sliding_window_sink_attention(q, k, v, sinks):
    B, S, HQ, D = q.shape
    HKV = k.shape[2]
    G = HQ // HKV
    nb = S // SWA_BLOCK
    scale = 1.0 / math.sqrt(D)

    qb = q.reshape(B, nb, SWA_BLOCK, HKV, G, D)

    def with_prev(t):
        tb = t.reshape(B, nb, SWA_BLOCK, HKV, D)
        prev = jnp.pad(tb[:, :-1], ((0, 0), (1, 0), (0, 0), (0, 0), (0, 0)))
        return jnp.concatenate([prev, tb], axis=2)

    kk = with_prev(k)
    vv = with_prev(v)
    s = jnp.einsum('bnqhgd,bnkhd->bnhgqk', qb, kk).astype(jnp.float32) * scale

    qi = jnp.arange(SWA_BLOCK)[:, None]
    ki = jnp.arange(2 * SWA_BLOCK)[None, :]
    rel = qi + SWA_BLOCK - ki
    band = (rel >= 0) & (rel < SWA_WINDOW)
    has_prev = (jnp.arange(nb)[:, None, None] > 0) | (ki >= SWA_BLOCK)[None]
    valid = band[None] & has_prev
    s = jnp.where(valid[None, :, None, None], s, NEG_INF)

    sink = jnp.broadcast_to(
        sinks.astype(jnp.float32).reshape(1, 1, HKV, G, 1, 1), s.shape[:-1] + (1,))
    p = jax.nn.softmax(jnp.concatenate([s, sink], axis=-1), axis=-1)[..., :-1]
    out = jnp.einsum('bnhgqk,bnkhd->bnqhgd', p.astype(v.dtype), vv)
    return out.reshape(B, S, HQ * D)


def hybrid_layer(x, ln_g, w_in, q_a_g, w_q_up, kv_a_g, w_kv_up, sinks, w_out,
                 cos_mla, sin_mla, cos_swa, sin_swa):
    B, S, _ = x.shape
    h = rms_norm(x, ln_g)
    proj = h @ w_in
    c_q, c_kv, k_rope, g_mla, q_s, k_s, v_s, g_swa = jnp.split(proj, IN_SPLITS, axis=-1)

    q = (rms_norm(c_q, q_a_g) @ w_q_up).reshape(B, S, MLA_HEADS, MLA_QK_DIM)
    q_nope = q[..., :MLA_NOPE_DIM]
    q_rope = apply_rope(q[..., MLA_NOPE_DIM:], cos_mla, sin_mla)
    k_rope = apply_rope(k_rope, cos_mla, sin_mla)
    kv = (rms_norm(c_kv, kv_a_g) @ w_kv_up).reshape(B, S, MLA_HEADS, MLA_NOPE_DIM + MLA_V_DIM)
    k_nope = kv[..., :MLA_NOPE_DIM]
    v_mla = kv[..., MLA_NOPE_DIM:]
    o_mla = mla_attention(q_nope, q_rope, k_nope, k_rope, v_mla)

    q_s = apply_rope(q_s.reshape(B, S, SWA_Q_HEADS, SWA_HEAD_DIM), cos_swa, sin_swa)
    k_s = apply_rope(k_s.reshape(B, S, SWA_KV_HEADS, SWA_HEAD_DIM), cos_swa, sin_swa)
    v_s = v_s.reshape(B, S, SWA_KV_HEADS, SWA_HEAD_DIM)
    o_swa = sliding_window_sink_attention(q_s, k_s, v_s, sinks)

    mixed = jnp.concatenate([o_mla * jax.nn.silu(g_mla), o_swa * jax.nn.silu(g_swa)], axis=-1)
    return x + mixed @ w_out


def setup_inputs(seed: int = 0) -> dict:
    key = jax.random.key(seed)
    ks = jax.random.split(key, 11)
    f32 = jnp.float32

    def w(k, shape, fan_in):
        return jax.random.normal(k, shape, f32) * (fan_in ** -0.5)

    def gain(k, shape):
        return 1.0 + 0.02 * jax.random.normal(k, shape, f32)

    return {
        "x": jax.random.normal(ks[0], (BATCH, SEQ, D_MODEL), f32),
        "ln_mix": gain(ks[1], (DEPTH, D_MODEL)),
        "w_in": w(ks[2], (DEPTH, D_MODEL, IN_WIDTH), D_MODEL),
        "q_a_norm": gain(ks[3], (DEPTH, MLA_Q_RANK)),
        "w_q_up": w(ks[4], (DEPTH, MLA_Q_RANK, MLA_HEADS * MLA_QK_DIM), MLA_Q_RANK),
        "kv_a_norm": gain(ks[5], (DEPTH, MLA_KV_RANK)),
        "w_kv_up": w(ks[6], (DEPTH, MLA_KV_RANK, MLA_HEADS * (MLA_NOPE_DIM + MLA_V_DIM)), MLA_KV_RANK),
        "attn_sinks": jax.random.normal(ks[7], (DEPTH, SWA_Q_HEADS), f32),
        "w_out": w(ks[8], (DEPTH, MIX_WIDTH, D_MODEL), MIX_WIDTH),
        "final_norm": gain(ks[9], (D_MODEL,)),
    }


def reference(x, ln_mix, w_in, q_a_norm, w_q_up, kv_a_norm, w_kv_up, attn_sinks, w_out, final_norm):
    S = x.shape[1]
    cos_mla, sin_mla = rope_tables(S, MLA_ROPE_DIM)
    cos_swa, sin_swa = rope_tables(S, SWA_HEAD_DIM)
    h = x
    for l in range(DEPTH):
        h = hybrid_layer(h, ln_mix[l], w_in[l], q_a_norm[l], w_q_up[l], kv_a_norm[l],
                         w_kv_up[l], attn_sinks[l], w_out[l],
                         cos_mla, sin_mla, cos_swa, sin_swa)
    return rms_norm(h, final_norm)
```

```python
import math
from contextlib import ExitStack

import numpy as np
import concourse.bass as bass
import concourse.mybir as mybir
from concourse.bass_utils import run_bass_kernel_spmd

F32 = mybir.dt.float32
BF16 = mybir.dt.bfloat16
AF = mybir.ActivationFunctionType
ALU = mybir.AluOpType

S = 2048
D = 1024
NT = 16
TB = 512
NTB = 4
EPS = 1e-6
NEG = -30000.0
SC_MLA = 1.0 / math.sqrt(192.0)
SC_SWA = 1.0 / math.sqrt(64.0)
W_IN_COLS = 2304

ENGS = ("pe", "act", "dve", "pool", "sp")


class Tok:
    __slots__ = ("name", "w", "rs", "excl")

    def __init__(self, name):
        self.name = name
        self.w = None
        self.rs = []
        self.excl = False


class Op:
    __slots__ = ("eng", "fn", "kind", "deps", "inc", "cnt", "chan", "val", "idx")

    def __init__(self, eng, fn, kind):
        self.eng = eng
        self.fn = fn
        self.kind = kind
        self.deps = []
        self.inc = False
        self.cnt = None
        self.chan = None
        self.val = None


class Sched:
    def __init__(self):
        self.ops = []
        self.by_eng = {e: [] for e in ENGS}
        self.chan_cnt = {}

    def op(self, eng, fn, reads=(), writes=(), chan=None):
        kind = "d" if chan is not None else "c"
        o = Op(eng, fn, kind)
        o.idx = len(self.ops)
        deps = {}
        ex = [t for t in reads if t.excl and t not in writes]
        if ex:
            writes = list(writes) + ex
        for t in reads:
            if t.w is not None:
                deps[t.w] = "raw"
        for t in writes:
            if t.w is not None and t.w not in deps:
                deps[t.w] = "waw"
            for r in t.rs:
                if r not in deps:
                    deps[r] = "war"
        for d, k in deps.items():
            if d is o:
                continue
            if d.kind == "c" and o.kind == "c" and d.eng == eng and eng == "pe":
                continue
            o.deps.append(d)
            if d.kind == "c":
                d.inc = True
        for t in reads:
            t.rs.append(o)
        for t in writes:
            t.w = o
            t.rs = []
        if kind == "d":
            self.chan_cnt[chan] = self.chan_cnt.get(chan, 0) + 16
            o.chan = chan
            o.val = self.chan_cnt[chan]
        self.ops.append(o)
        self.by_eng[eng].append(o)
        return o

    def finalize(self):
        cnt = {e: 0 for e in ENGS}
        for o in self.ops:
            if o.kind == "c" and o.inc:
                cnt[o.eng] += 1
                o.cnt = cnt[o.eng]

    def emit(self, eng, engobj, sems, chan_sems):
        seen = {}
        for o in self.by_eng[eng]:
            waits = {}
            for d in o.deps:
                if d.kind == "c":
                    key, sem, val = ("e", d.eng), sems[d.eng], d.cnt
                else:
                    key, sem, val = ("c", d.chan), chan_sems[d.chan], d.val
                if key not in waits or waits[key][1] < val:
                    waits[key] = (sem, val)
            for key, (sem, val) in waits.items():
                if seen.get(key, 0) < val:
                    engobj.wait_ge(sem, val)
                    seen[key] = val
            ins = o.fn(engobj)
            if o.kind == "c":
                if o.inc:
                    ins.then_inc(sems[eng], 1)
            else:
                ins.then_inc(chan_sems[o.chan], 16)


class _Stop(Exception):
    pass


def build_nc(debug=None, limit=99, nblocks=NTB, chunks=None, dovs=True):
    nc = bass.Bass("TRN2", target_bir_lowering=False)
    sch = Sched()
    es = ExitStack()

    def dram_in(name, shape):
        return nc.dram_tensor(name, list(shape), F32, kind="ExternalInput").ap()

    x_d = dram_in("x", (S, D))
    w_in_d = dram_in("w_in", (D, 2240))
    w_qup_d = dram_in("w_qup", (256, 768))
    w_kvup_d = dram_in("w_kvup", (128, 1024))
    w_out_d = dram_in("w_out", (D, D))
    lng_d = dram_in("lng", (128, 8))
    qag_d = dram_in("qag", (128, 2))
    kvag_d = dram_in("kvag", (128, 1))
    fing_d = dram_in("fing", (128, D))
    sink_d = dram_in("sinkbc", (128, 4))
    cos_d = dram_in("cosT", (128, S))
    sin_d = dram_in("sinT", (128, S))
    cst_d = dram_in("cst", (128, 1024))
    msw_d = dram_in("msw", (128, 1024))
    out_d = nc.dram_tensor("out", [S, D], F32, kind="ExternalOutput").ap()
    dbg_d = None
    if debug:
        dbg_d = nc.dram_tensor("dbg", [128, debug["cols"]], F32, kind="ExternalOutput").ap()

    def sb(name, shape, dt):
        return es.enter_context(nc.sbuf_tensor(name, list(shape), dt))

    def ps(name, shape, dt):
        return es.enter_context(nc.psum_tensor(name, list(shape), dt))

    w_in_bf = sb("w_in_bf", (128, 8, W_IN_COLS), BF16)
    w_out_bf = sb("w_out_bf", (128, 8, D), BF16)
    w_qup_bf = sb("w_qup_bf", (128, 2, 1024), BF16)
    w_kvup_bf = sb("w_kvup_bf", (128, 1024), BF16)
    wst = [sb(f"wst{i}", (128, 1120), F32) for i in range(2)]
    ident_bf = sb("ident_bf", (128, 128), BF16)
    rm_bf = sb("rm_bf", (128, 128), BF16)
    maskc_bf = sb("maskc_bf", (128, 128), BF16)
    msw_bf = sb("msw_bf", (128, 2, 512), BF16)
    ones_bf = sb("ones_bf", (128, 128), BF16)
    onesg_bf = sb("onesg_bf", (128, 2, 128), BF16)
    cosT = sb("cosT_s", (128, TB), F32)
    sinT = sb("sinT_s", (128, TB), F32)
    lng = sb("lng_s", (128, 8), F32)
    qag = sb("qag_s", (128, 2), F32)
    kvag = sb("kvag_s", (128, 1), F32)
    fing = sb("fing_s", (128, D), F32)
    sinkraw = sb("sinkraw", (128, 4), F32)
    sinkexp = sb("sinkexp", (128, 4), F32)
    sinkbc = sb("sinkbc_s", (128, 4, 128), F32)
    neghalf = sb("neghalf", (128, 128), F32)
    epsc = sb("epsc", (128, 1), F32)

    KnT = sb("KnT", (128, 4, S), BF16)
    KrT = sb("KrT", (128, S), BF16)
    Vm = sb("Vm", (128, NT, 512), BF16)
    KsT = sb("KsT", (128, 2, 640), BF16)
    Vs = sb("Vs", (128, 5, 2, 128), BF16)

    xst = [sb(f"xst{i}", (128, D), F32) for i in range(2)]
    xs_bf = [sb(f"xs_bf{i}", (128, D), BF16) for i in range(2)]
    ssq = sb("ssq", (128, NT), F32)
    tmp4 = sb("tmp4", (128, NT), F32)
    rstd4 = sb("rstd4", (128, NT), F32)
    hT = sb("hT", (128, 8, TB), BF16)
    hT2 = sb("hT2", (128, 8, TB), BF16)
    cq_f = sb("cq_f", (128, 3, TB), F32)
    sq_bf = sb("sq_bf", (128, 3, TB), BF16)
    tstat = sb("tstat", (128, 2, TB), F32)
    rstat = tstat
    cn_bf = sb("cn_bf", (128, 3, TB), BF16)
    QnT = sb("QnT", (128, 4, TB), BF16)
    QrT = sb("QrT", (128, 4, TB), BF16)
    QsT = sb("QsT", (128, 4, TB), BF16)
    gate = sb("gate", (128, 8, TB), BF16)
    abf = [sb(f"abf{i}", (128, TB), BF16) for i in range(2)]
    t1 = [sb(f"t1_{i}", (128, TB), F32) for i in range(2)]
    t2 = [sb(f"t2_{i}", (128, TB), F32) for i in range(2)]
    th = [sb("th0", (128, TB), F32)]
    PT = [sb(f"PT{i}", (128, TB), BF16) for i in range(4)]
    rr = [sb(f"rr{i}", (128, TB), F32) for i in range(2)]
    on = rr
    yb = [wst[i][:, 0:D] for i in range(2)]
    ss2 = sb("ss2", (128, NT), F32)
    tmp2 = sb("tmp2", (128, NT), F32)
    rstd2 = sb("rstd2", (128, NT), F32)
    tl = sb("tl", (128, 2), F32)

    psA = [ps(f"psA{i}", (128, 512), F32) for i in range(4)]
    psO = [ps(f"psO{i}", (128, 512), F32) for i in range(2)]
    psL = [ps(f"psL{i}", (128, 512), F32) for i in range(2)]

    class T:
        pass

    tk = {}

    def tok(name):
        if name not in tk:
            tk[name] = Tok(name)
        return tk[name]

    rot = {"A": 0, "abf": 0, "t": 0, "th": 0, "PT": 0, "x": 0, "w": 0, "O": 0, "rr": 0, "y": 0}

    def nxt(key, n):
        v = rot[key]
        rot[key] = (v + 1) % n
        return v

    def dma(out_ap, in_ap, reads, writes, chan):
        sch.op("sp", lambda e: e.dma_start(out=out_ap, in_=in_ap), reads=reads, writes=writes, chan=chan)

    def act(out_ap, in_ap, func, reads, writes, scale=None, accum=None, bias=None):
        def fn(e):
            kw = {}
            if bias is not None:
                kw["bias"] = bias
            if scale is not None:
                kw["scale"] = scale
            if accum is not None:
                kw["accum_out"] = accum
            return e.activation(out_ap, in_ap, func, **kw)
        sch.op("act", fn, reads=reads, writes=writes)

    def vop(eng, fn, reads, writes):
        sch.op(eng, fn, reads=reads, writes=writes)

    def mm_group(mms, reads, writes):
        def fn(e):
            ins = None
            for (o, l, r, st, sp_) in mms:
                ins = e.matmul(o, l, r, start=st, stop=sp_)
            return ins
        sch.op("pe", fn, reads=reads, writes=writes)

    cast_engs = ["act", "dve"]

    def cast_scaled(eng, out_ap, in_ap, scale_ap, reads, writes):
        if eng == "act":
            if scale_ap is None:
                act(out_ap, in_ap, AF.Copy, reads, writes)
            else:
                act(out_ap, in_ap, AF.Identity, reads, writes, scale=scale_ap)
        else:
            if scale_ap is None:
                vop(eng, lambda e: e.tensor_copy(out_ap, in_ap), reads, writes)
            else:
                vop(eng, lambda e: e.tensor_scalar(out_ap, in_ap, scale_ap, None, ALU.mult), reads, writes)

    vm_f = Vm[:].rearrange("p a b -> p (a b)").bitcast(F32)
    kn_f = KnT[:].rearrange("p a b -> p (a b)").bitcast(F32)
    NSL = 8
    wsl = [wst[0][:, :], wst[1][:, :]] + [vm_f[:, i * 1120:(i + 1) * 1120] for i in range(3)] + \
          [kn_f[:, i * 1120:(i + 1) * 1120] for i in range(3)]
    wtok = [tok(f"wst{i}") for i in range(NSL)]

    def stage(dram_ap, ncols):
        sl = nxt("w", NSL)
        dma(wsl[sl][:, 0:ncols], dram_ap, [], [wtok[sl]], f"wst{sl}")
        return sl

    def inherit(dst, srcs):
        for s_ in srcs:
            if s_.w is not None:
                dst.rs.append(s_.w)
            dst.rs.extend(s_.rs)

    x_tok = [tok(f"xst{i}") for i in range(2)]

    def load_x(t):
        sl = nxt("x", 2)
        dma(xst[sl][:], x_d[t * 128:(t + 1) * 128, :], [], [x_tok[sl]], f"xst{sl}")
        return sl

    xslots0 = [load_x(0), load_x(1)]

    sl = stage(cst_d[:, :], 1024)
    vop("dve", (lambda sl: lambda e: e.tensor_copy(ident_bf[:], wsl[sl][:, 0:128]))(sl), [wtok[sl]], [tok("ident")])
    vop("dve", (lambda sl: lambda e: e.tensor_copy(rm_bf[:], wsl[sl][:, 128:256]))(sl), [wtok[sl]], [tok("rm")])
    vop("dve", (lambda sl: lambda e: e.tensor_copy(maskc_bf[:], wsl[sl][:, 256:384]))(sl), [wtok[sl]], [tok("maskc")])
    vop("pool", lambda e: e.memset(ones_bf[:], 1.0), [], [tok("ones")])
    vop("pool", lambda e: e.memset(onesg_bf[:], 0.0), [], [tok("onesg")])
    vop("pool", lambda e: e.memset(onesg_bf[:, 0, 0:64], 1.0), [], [tok("onesg")])
    vop("pool", lambda e: e.memset(onesg_bf[:, 1, 64:128], 1.0), [], [tok("onesg")])
    vop("pool", lambda e: e.memset(neghalf[:], -0.5), [], [tok("neghalf")])
    vop("pool", lambda e: e.memset(epsc[:], EPS), [], [tok("epsc")])
    vop("pool", lambda e: e.memset(KsT[:], 0.0), [], [tok("KsT")])
    vop("pool", lambda e: e.memset(Vs[:], 0.0), [], [tok("Vs")])
    vop("pool", lambda e: e.memset(w_in_bf[:, :, 448:512], 0.0), [], [tok("w_in_pad")])
    vop("pool", lambda e: e.memset(w_qup_bf[:, :, 512:1024], 0.0), [], [tok("w_qup_pad")])

    dma(lng[:], lng_d[:, :], [], [tok("lng")], "lng")
    dma(qag[:], qag_d[:, :], [], [tok("qag")], "qag")
    dma(kvag[:], kvag_d[:, :], [], [tok("kvag")], "kvag")
    dma(sinkraw[:], sink_d[:, :], [], [tok("sinkraw")], "sink")

    t_wq = tok("w_qup_bf")
    t_wkv = tok("w_kvup_bf")
    wout_slots = []

    t_wh0 = [tok(f"w_in_h0_{k}{p}") for k in range(8) for p in "ab"]
    t_wh1 = [tok(f"w_in_h1_{k}{p}") for k in range(8) for p in "ab"]
    h1_slots = {}
    small_slots = {}

    def stage_half0(ks):
        ci = 0
        for k in ks:
            sl = stage(w_in_d[k * 128:(k + 1) * 128, 0:1120], 1120)
            g = lng[:, k:k + 1]
            rd = [wtok[sl], tok("lng"), tok("w_in_pad")]
            cast_scaled(cast_engs[ci % 2], w_in_bf[:, k, 0:448], wsl[sl][:, 0:448], g, rd, [tok(f"w_in_h0_{k}a")])
            ci += 1
            cast_scaled(cast_engs[ci % 2], w_in_bf[:, k, 512:1184], wsl[sl][:, 448:1120], g, rd, [tok(f"w_in_h0_{k}b")])
            ci += 1

    def stage_half1_dmas():
        for k in range(8):
            h1_slots[k] = stage(w_in_d[k * 128:(k + 1) * 128, 1120:2240], 1120)

    def cast_half1(k):
        sl = h1_slots[k]
        g = lng[:, k:k + 1]
        rd = [wtok[sl], tok("lng")]
        cast_scaled("act", w_in_bf[:, k, 1184:1744], wsl[sl][:, 0:560], g, rd, [tok(f"w_in_h1_{k}a")])
        cast_scaled("dve", w_in_bf[:, k, 1744:2304], wsl[sl][:, 560:1120], g, rd, [tok(f"w_in_h1_{k}b")])

    def later_dma(i):
        if i == 0:
            small_slots["q0"] = stage(w_qup_d[0:128, :], 768)
        elif i == 1:
            small_slots["q1"] = stage(w_qup_d[128:256, :], 768)
        elif i == 2:
            small_slots["kv"] = stage(w_kvup_d[:, :], 1024)
        elif i == 3:
            small_slots["msw"] = stage(msw_d[:, :], 1024)
        else:
            k = i - 4
            wout_slots.append(stage(w_out_d[k * 128:(k + 1) * 128, :], 1024))

    def cast_small():
        for k in range(2):
            sl = small_slots[f"q{k}"]
            cast_scaled("dve", w_qup_bf[:, k, 0:512], wsl[sl][:, 0:512], qag[:, k:k + 1],
                        [wtok[sl], tok("qag"), tok("w_qup_pad")], [t_wq])
            dst = w_qup_bf[:, k, 512:1024].rearrange("p (h c) -> p h c", c=128)[:, :, 0:64]
            srcv = wsl[sl][:, 512:768].rearrange("p (h c) -> p h c", c=64)
            cast_scaled("dve", dst, srcv, qag[:, k:k + 1], [wtok[sl], tok("qag"), tok("w_qup_pad")], [t_wq])
        sl = small_slots["kv"]
        cast_scaled("act", w_kvup_bf[:, :], wsl[sl][:, 0:1024], kvag[:, 0:1], [wtok[sl], tok("kvag")], [t_wkv])
        sl = small_slots["msw"]
        cast_scaled("dve", msw_bf[:].rearrange("p a b -> p (a b)"), wsl[sl][:, 0:1024], None, [wtok[sl]], [tok("msw")])

    t_wout = [tok(f"w_out_bf{k}") for k in range(8)]
    ytok = [tok("yb0"), tok("yb1")]


    tA = [tok(f"psA{i}") for i in range(4)]
    tO = [tok(f"psO{i}") for i in range(2)]
    tL = [tok(f"psL{i}") for i in range(2)]
    for t_ in tA + tO + tL:
        t_.excl = True

    evac_rr = [0]

    def evac_eng():
        evac_rr[0] = (evac_rr[0] + 1) % 3
        return "dve" if evac_rr[0] == 0 else "act"

    def copy_evac(eng, out_ap, in_ap, reads, writes):
        if eng == "act":
            act(out_ap, in_ap, AF.Copy, reads, writes)
        else:
            vop(eng, lambda e: e.tensor_copy(out_ap, in_ap), reads, writes)

    def rope_from_bank(a, dests, writes):
        sa = nxt("abf", 2)
        act(abf[sa][:], psA[a][:], AF.Copy, [tA[a]], [tok(f"abf{sa}")])

        def part2():
            b = nxt("A", 4)
            mm_group([(psA[b][:], rm_bf[:], abf[sa][:], True, True)], [tok("rm"), tok(f"abf{sa}")], [tA[b]])
            st = nxt("t", 2)
            vop("dve", lambda e: e.tensor_tensor(t1[st][:], psA[a][:], cosT[:], ALU.mult),
                [tA[a], tok("cos")], [tok(f"t1_{st}")])
            vop("dve", lambda e: e.tensor_tensor(t2[st][:], psA[b][:], sinT[:], ALU.mult),
                [tA[b], tok("sin")], [tok(f"t2_{st}")])
            for (dap, psl) in dests:
                vop("pool", (lambda dap, psl: lambda e: e.tensor_tensor(dap, t1[st][psl, :], t2[st][psl, :], ALU.add))(dap, psl),
                    [tok(f"t1_{st}"), tok(f"t2_{st}")], writes)
        return part2

    pend = []

    def flush_pending():
        while pend:
            pend.pop(0)()

    defer = []

    def tick_defer(force=False):
        keep = []
        for item in defer:
            item[0] -= 1
            if force or item[0] <= 0:
                item[1]()
            else:
                keep.append(item)
        defer[:] = keep

    def run_pipeline(units, lag=2):
        n = len(units)
        due = {}
        for i in range(n):
            units[i]["qk"](units[i])
            due.setdefault(i + units[i].get("lag", lag), []).append(i)
            for j in due.pop(i, []):
                units[j]["pv"](units[j])
        for k in sorted(due):
            for j in due[k]:
                units[j]["pv"](units[j])

    def recip_act(r, l_ap, reads):
        act(rr[r][:], l_ap, AF.Ln, reads, [tok(f"rr{r}")])
        act(rr[r][:], rr[r][:], AF.Exp, [tok(f"rr{r}")], [tok(f"rr{r}")], scale=-1.0)

    def gate_from_bank(a, dest_ap, writes):
        s_ = nxt("th", 1)
        act(th[s_][:], psA[a][:], AF.Tanh, [tA[a]], [tok(f"th{s_}")], scale=0.5)
        vop("dve", lambda e: e.scalar_tensor_tensor(dest_ap, th[s_][:], 1.0, psA[a][:], ALU.add, ALU.mult),
            [tok(f"th{s_}"), tA[a]], writes)

    def chk(k):
        if limit < k:
            raise _Stop()

    HT = [hT, hT2]

    def blk_vars(tb):
        c0 = tb * TB
        return c0, slice(c0, c0 + TB), tok(f"hT{tb % 2}"), HT[tb % 2]

    def front_tile_unit(tb, j, lag):
        c0, tcols, t_hT, hT = blk_vars(tb)
        t = 4 * tb + j

        def qk(U):
            if tb == 0 and j < 2:
                sl = xslots0[j]
            else:
                sl = load_x(t)
            U["sl"] = sl
            vop("dve", lambda e: e.scalar_tensor_tensor(xs_bf[sl][:], xst[sl][:], 1.0, xst[sl][:], ALU.mult, ALU.mult,
                                                        accum_out=ssq[:, t:t + 1]),
                [x_tok[sl]], [tok(f"xs_bf{sl}"), tok(f"ssq{t}")])
            act(tmp4[:, t:t + 1], ssq[:, t:t + 1], AF.Ln, [tok(f"ssq{t}"), tok("epsc")], [tok(f"tmp4_{t}")], scale=1.0 / D, bias=epsc[:, 0:1])
            act(rstd4[:, t:t + 1], tmp4[:, t:t + 1], AF.Exp, [tok(f"tmp4_{t}")], [tok(f"rstd4_{t}")], scale=-0.5)
            vop("dve", lambda e: e.tensor_scalar(xs_bf[sl][:], xst[sl][:], rstd4[:, t:t + 1], None, ALU.mult),
                [x_tok[sl], tok(f"rstd4_{t}")], [tok(f"xs_bf{sl}")])

        def pvf(U):
            sl = U["sl"]
            a = nxt("A", 4)
            psT = psA[a][:].bitcast(BF16)

            def tr_fn(e):
                ins = None
                for kc in range(8):
                    ins = e.transpose(psT[:, kc * 128:(kc + 1) * 128], xs_bf[sl][:, kc * 128:(kc + 1) * 128], ident_bf[:])
                return ins
            sch.op("pe", tr_fn, reads=[tok(f"xs_bf{sl}"), tok("ident")], writes=[tA[a]])
            copy_evac("act", hT[:, :, j * 128:(j + 1) * 128], psT.rearrange("p (k c) -> p k c", c=128),
                      [tA[a]], [t_hT])
        return {"qk": qk, "pv": pvf, "lag": lag}

    def sec_front(tb):
        run_pipeline([front_tile_unit(tb, j, 1) for j in range(4)])

    def sec_inproj_pre(tb):
        c0, tcols, t_hT, hT = blk_vars(tb)
        dma(cosT[:], cos_d[:, tcols], [], [tok("cos")], "cos")
        dma(sinT[:], sin_d[:, tcols], [], [tok("sin")], "sin")
        if tb > 0:
            vop("pool", lambda e: e.tensor_copy(KsT[:, :, 0:128], KsT[:, :, 512:640]), [tok("KsT")], [tok("KsT")])
            vop("pool", lambda e: e.tensor_copy(Vs[:, 0, :, :], Vs[:, 4, :, :]), [tok("Vs")], [tok("Vs")])

    def inproj_kouter(cs):
        c0, tcols, t_hT, hT = blk_vars(0)
        banks = {c: nxt("A", 4) for c in cs}
        for kc in range(8):
            for c in cs:
                a = banks[c]
                mm_group([(psA[a][:], w_in_bf[:, kc, c * 128:(c + 1) * 128], hT[:, kc, :], kc == 0, kc == 7)],
                         [t_hT, tok(f"w_in_h0_{kc}a")], [tA[a]])
        return banks

    def sec_inproj_chunks(tb, cs, pre_banks=None):
        c0, tcols, t_hT, hT = blk_vars(tb)

        def inproj_chunk(c):
            if pre_banks is not None and c in pre_banks:
                return pre_banks[c]
            a = nxt("A", 4)
            mm_group([(psA[a][:], w_in_bf[:, kc, c * 128:(c + 1) * 128], hT[:, kc, :], kc == 0, kc == 7)
                      for kc in range(8)], [t_hT] + t_wh0 + (t_wh1 if c > 8 else []), [tA[a]])
            return a

        for c in cs:
            a = inproj_chunk(c)
            flush_pending()
            if c <= 2:
                act(sq_bf[:, c, :], psA[a][:], AF.Square, [tA[a]], [tok(f"sq_bf{c}")])
                vop("dve", (lambda a, c: lambda e: e.tensor_copy(cq_f[:, c, :], psA[a][:]))(a, c), [tA[a]], [tok(f"cq_f{c}")])
            elif c == 3:
                pend.append(rope_from_bank(a, [(KrT[:, tcols], slice(0, 128))], [tok("KrT")]))
            elif c <= 7:
                gate_from_bank(a, gate[:, c - 4, :], [tok(f"gate{c - 4}")])
            elif c <= 11:
                pend.append(rope_from_bank(a, [(QsT[:, c - 8, :], slice(0, 128))], [tok("QsT")]))
            elif c == 12:
                pend.append(rope_from_bank(a, [(KsT[0:64, 0, 128:640], slice(0, 64)), (KsT[64:128, 1, 128:640], slice(64, 128))],
                                           [tok("KsT")]))
            else:
                gate_from_bank(a, gate[:, 4 + (c - 13), :], [tok(f"gate{4 + c - 13}")])

    def sec_inproj_post(tb):
        c0, tcols, t_hT, hT = blk_vars(tb)
        a = nxt("A", 4)
        mms = []
        for j in range(4):
            for kc in range(8):
                mms.append((psA[a][:, j * 128:(j + 1) * 128], hT[:, kc, j * 128:(j + 1) * 128],
                            w_in_bf[:, kc, 2176:2304], kc == 0, kc == 7))
        mm_group(mms, [t_hT] + t_wh0 + t_wh1, [tA[a]])
        flush_pending()
        pv = psA[a][:].rearrange("p (j c) -> p j c", c=128)
        vop("dve", (lambda pv: lambda e: e.tensor_copy(Vs[:, 1:5, 0, 0:64], pv[:, :, 0:64]))(pv), [tA[a]], [tok("Vs")])
        vop("dve", (lambda pv: lambda e: e.tensor_copy(Vs[:, 1:5, 1, 64:128], pv[:, :, 64:128]))(pv), [tA[a]], [tok("Vs")])

        if tb == 0:
            inherit(ytok[0], [wtok[0]])
            inherit(ytok[1], [wtok[1]])
            inherit(tok("Vm"), wtok[2:5])
            inherit(tok("KnT"), wtok[5:8])

    def sec_stats(tb):
        c0, tcols, t_hT, hT = blk_vars(tb)
        a = nxt("A", 4)
        mm_group([(psA[a][:], ones_bf[:], sq_bf[:, 0, :], True, False),
                  (psA[a][:], ones_bf[:], sq_bf[:, 1, :], False, True)],
                 [tok("ones"), tok("sq_bf0"), tok("sq_bf1")], [tA[a]])
        act(tstat[:, 0, :], psA[a][:], AF.Ln, [tA[a], tok("epsc")], [tok("tstat0")], scale=1.0 / 256.0, bias=epsc[:, 0:1])
        a = nxt("A", 4)
        mm_group([(psA[a][:], ones_bf[:], sq_bf[:, 2, :], True, True)], [tok("ones"), tok("sq_bf2")], [tA[a]])
        act(tstat[:, 1, :], psA[a][:], AF.Ln, [tA[a], tok("epsc")], [tok("tstat1")], scale=1.0 / 128.0, bias=epsc[:, 0:1])
        for q in range(2):
            act(rstat[:, q, :], tstat[:, q, :], AF.Exp, [tok(f"tstat{q}")], [tok(f"tstat{q}")], scale=-0.5)
        for c in range(3):
            q = 0 if c < 2 else 1
            vop("dve", (lambda c, q: lambda e: e.tensor_tensor(cn_bf[:, c, :], cq_f[:, c, :], rstat[:, q, :], ALU.mult))(c, q),
                [tok(f"cq_f{c}"), tok(f"tstat{q}")], [tok(f"cn_bf{c}")])


    def sec_swa(tb):
        c0, tcols, t_hT, hT = blk_vars(tb)
        swa_units = []
        for jq in range(4):
            qt = 4 * tb + jq
            grp = {"o": None}
            ul = [(g, ks) for ks in (jq, jq + 1) if not (qt == 0 and ks == 0) for g in (0, 1)]
            qcols = slice(jq * 128, (jq + 1) * 128)
            for u, (g, ks) in enumerate(ul):
                def qk(U, g=g, ks=ks, jq=jq, qcols=qcols):
                    a = nxt("A", 4)
                    mi = 0 if ks == jq + 1 else 1
                    mm_group([(psA[a][:], KsT[:, g, ks * 128:(ks + 1) * 128], QsT[:, :, qcols], True, False),
                              (psA[a][:], ident_bf[:], msw_bf[:, mi, :], False, True)],
                             [tok("KsT"), tok("QsT"), tok("ident"), tok("msw")], [tA[a]])
                    p = nxt("PT", 4)
                    act(PT[p][:], psA[a][:], AF.Exp, [tA[a]], [tok(f"PT{p}")], scale=SC_SWA)
                    U["p"] = p

                def pvf(U, g=g, ks=ks, first=(u == 0), last=(u == len(ul) - 1), grp=grp, qcols=qcols):
                    tick_defer()
                    if first:
                        grp["o"] = nxt("O", 2)
                    o = grp["o"]
                    p = U["p"]
                    mm_group([(psO[o][:], Vs[:, ks, g, :], PT[p][:], first, last),
                              (psL[o][:], onesg_bf[:, g, :], PT[p][:], first, last)],
                             [tok("Vs"), tok("onesg"), tok(f"PT{p}")], [tO[o], tL[o]])
                    if last:
                        def fin():
                            r = nxt("rr", 2)
                            vop("dve", lambda e: e.tensor_tensor(rr[r][:], psL[o][:], sinkbc[:].rearrange("p a b -> p (a b)"), ALU.add),
                                [tL[o], tok("sinkbc")], [tok(f"rr{r}")])
                            recip_act(r, rr[r][:], [tok(f"rr{r}")])
                            vop("dve", lambda e: e.scalar_tensor_tensor(rr[r][:], psO[o][:], 0.5, rr[r][:], ALU.mult, ALU.mult),
                                [tO[o], tok(f"rr{r}")], [tok(f"rr{r}")])
                            gts = [tok(f"gate{4 + i}") for i in range(4)]
                            vop("pool", lambda e: e.tensor_tensor(gate[:, 4:8, qcols], rr[r][:].rearrange("p (i q) -> p i q", q=128),
                                                                  gate[:, 4:8, qcols], ALU.mult),
                                [tok(f"rr{r}")] + gts, gts)
                        defer.append([1, fin])
                swa_units.append({"qk": qk, "pv": pvf})
        run_pipeline(swa_units)
        tick_defer(force=True)


    def keep_warm(n=1):
        od = rot["O"]
        mm_group([(psL[od][:], ones_bf[:], w_kvup_bf[:, 0:512], True, True)] * n, [t_wkv, tok("ones")], [tL[od]])

    def sec_upproj(tb):
        c0, tcols, t_hT, hT = blk_vars(tb)
        for ch in range(8):
            a = nxt("A", 4)
            mm_group([(psA[a][:], w_qup_bf[:, kc, ch * 128:(ch + 1) * 128], cn_bf[:, kc, :], kc == 0, kc == 1)
                      for kc in range(2)], [t_wq, tok("cn_bf0"), tok("cn_bf1")], [tA[a]])
            flush_pending()
            keep_warm()
            if ch < 4:
                copy_evac(evac_eng(), QnT[:, ch, :], psA[a][:], [tA[a]], [tok("QnT")])
            else:
                pend.append(rope_from_bank(a, [(QrT[:, ch - 4, :], slice(0, 128))], [tok("QrT")]))
        for h in range(4):
            a = nxt("A", 4)
            mm_group([(psA[a][:], w_kvup_bf[:, h * 128:(h + 1) * 128], cn_bf[:, 2, :], True, True)],
                     [t_wkv, tok("cn_bf2")], [tA[a]])
            flush_pending()
            keep_warm()
            copy_evac(evac_eng(), KnT[:, h, tcols], psA[a][:], [tA[a]], [tok("KnT")])
        for j in range(4):
            a = nxt("A", 4)
            mm_group([(psA[a][:], cn_bf[:, 2, j * 128:(j + 1) * 128], w_kvup_bf[:, 512:1024], True, True)],
                     [t_wkv, tok("cn_bf2")], [tA[a]])
            keep_warm()
            copy_evac(evac_eng(), Vm[:, 4 * tb + j, :], psA[a][:], [tA[a]], [tok("Vm")])


    def sec_mla(tb):
        c0, tcols, t_hT, hT = blk_vars(tb)
        nk = 4 * tb + 4
        mla_units = []
        for h in range(4):
            grp = {"o": None}
            for kt in range(nk):
                def qk(U, h=h, kt=kt, tb=tb):
                    j = kt - 4 * tb
                    qlo = max(0, j) * 128
                    a = nxt("A", 4)
                    kcols = slice(kt * 128, (kt + 1) * 128)
                    mms = [(psA[a][:, qlo:512], KnT[:, h, kcols], QnT[:, h, qlo:512], True, False),
                           (psA[a][:, qlo:512], KrT[:, kcols], QrT[:, h, qlo:512], False, j < 0)]
                    if j >= 0:
                        mms.append((psA[a][:, qlo:qlo + 128], ident_bf[:], maskc_bf[:], False, True))
                    mm_group(mms, [tok("KnT"), tok("KrT"), tok("QnT"), tok("QrT"), tok("ident"), tok("maskc")], [tA[a]])
                    p = nxt("PT", 4)
                    act(PT[p][:, qlo:512], psA[a][:, qlo:512], AF.Exp, [tA[a]], [tok(f"PT{p}")], scale=SC_MLA)
                    U["p"] = p
                    U["qlo"] = qlo

                def pvf(U, h=h, kt=kt, first=(kt == 0), last=(kt == nk - 1), grp=grp):
                    if first:
                        grp["o"] = nxt("O", 2)
                    o = grp["o"]
                    p = U["p"]
                    qlo = U["qlo"]
                    mm_group([(psO[o][:, qlo:512], Vm[:, kt, h * 128:(h + 1) * 128], PT[p][:, qlo:512], first, last),
                              (psL[o][:, qlo:512], ones_bf[:], PT[p][:, qlo:512], first, last)],
                             [tok("Vm"), tok("ones"), tok(f"PT{p}")], [tO[o], tL[o]])
                    if last:
                        r = nxt("rr", 2)
                        recip_act(r, psL[o][:], [tL[o]])
                        vop("dve", lambda e: e.scalar_tensor_tensor(rr[r][:], psO[o][:], 0.5, rr[r][:], ALU.mult, ALU.mult),
                            [tO[o], tok(f"rr{r}")], [tok(f"rr{r}")])
                        vop("pool", lambda e: e.tensor_tensor(gate[:, h, :], rr[r][:], gate[:, h, :], ALU.mult),
                            [tok(f"rr{r}"), tok(f"gate{h}")], [tok(f"gate{h}")])
                mla_units.append({"qk": qk, "pv": pvf})
        if tb + 1 < nblocks:
            n_u = len(mla_units)
            step = max(1, (n_u // 2) // 4) if n_u > 16 else 4
            lagf = 6 if n_u > 16 else 3
            for j in range(4):
                mla_units.insert(min(len(mla_units), j * (step + 1)), front_tile_unit(tb + 1, j, lagf))
        mla_units.insert(min(len(mla_units), 3), {"qk": (lambda U: keep_warm(6)), "pv": (lambda U: None), "lag": 0})
        run_pipeline(mla_units)


    obuf = [(xst[0][:], x_tok[0], "xst0"), (xst[1][:], x_tok[1], "xst1"), (yb[0], ytok[0], "yb0"), (yb[1], ytok[1], "yb1")]

    def sec_outproj(tb):
        c0, tcols, t_hT, hT = blk_vars(tb)
        gall = [tok(f"gate{i}") for i in range(8)]
        fin_q = []
        for j in range(4):
            t = 4 * tb + j
            ob, otk, och = obuf[j]
            dma(ob, x_d[t * 128:(t + 1) * 128, :], [], [otk], och)
        korder = [4, 5, 6, 7, 0, 1, 2, 3]
        pre = {}
        if tb == nblocks - 1:
            gA = [tok(f"gate{i}") for i in korder[:7]]
            for j in range(2):
                for half in range(2):
                    a = nxt("A", 4)
                    hc = slice(half * 512, (half + 1) * 512)
                    mm_group([(psA[a][:], gate[:, kc, j * 128:(j + 1) * 128], w_out_bf[:, kc, hc], i == 0, False)
                              for i, kc in enumerate(korder[:7])], gA + t_wout, [tA[a]])
                    pre[(j, half)] = a
        for j in range(4):
            t = 4 * tb + j
            rows = slice(t * 128, (t + 1) * 128)
            ob, otk, och = obuf[j]
            for half in range(2):
                hc = slice(half * 512, (half + 1) * 512)
                if (j, half) in pre:
                    a = pre[(j, half)]
                    mm_group([(psA[a][:], gate[:, 3, j * 128:(j + 1) * 128], w_out_bf[:, 3, hc], False, True)],
                             [tok("gate3")] + t_wout, [tA[a]])
                else:
                    a = nxt("A", 4)
                    mm_group([(psA[a][:], gate[:, kc, j * 128:(j + 1) * 128], w_out_bf[:, kc, hc], kc == 0, kc == 7)
                              for kc in range(8)], gall + t_wout, [tA[a]])
                lastt = (tb == nblocks - 1 and j == 3)
                if lastt and half == 1:
                    vop("dve", (lambda a, ob, hc: lambda e: e.tensor_tensor(ob[:, hc], psA[a][:], ob[:, hc], ALU.add))(a, ob, hc),
                        [tA[a], otk], [tok("lastH1")])
                else:
                    vop("dve", (lambda a, ob, hc: lambda e: e.tensor_tensor(ob[:, hc], psA[a][:], ob[:, hc], ALU.add))(a, ob, hc),
                        [tA[a], otk], [otk])
                if half == 0:
                    flush_pending()
                    if lastt:
                        act(th[0][:].bitcast(BF16)[:, 0:512], ob[:, 0:512], AF.Square, [otk], [tok("th0"), tok("tlA")],
                            accum=tl[:, 0:1])
                        vop("dve", lambda e: e.tensor_scalar(tl[:, 1:2], tl[:, 0:1], 1.0 / D, EPS, ALU.mult, ALU.add),
                            [tok("tlA")], [tok("tlB")])
            if lastt:
                act(th[0][:].bitcast(BF16)[:, 512:1024], ob[:, 512:1024], AF.Square, [tok("lastH1")],
                    [tok("th0"), tok(f"ss2_{t}")], accum=ss2[:, t:t + 1])
                act(tmp2[:, t:t + 1], ss2[:, t:t + 1], AF.Ln, [tok(f"ss2_{t}"), tok("tlB")], [tok(f"tmp2_{t}")],
                    scale=1.0 / D, bias=tl[:, 1:2])
                act(rstd2[:, t:t + 1], tmp2[:, t:t + 1], AF.Exp, [tok(f"tmp2_{t}")], [tok(f"rstd2_{t}")], scale=-0.5)

                def fin_last(t=t, ob=ob, otk=otk, rows=rows, j=j):
                    vop("dve", lambda e: e.scalar_tensor_tensor(ob[:, 0:512], ob[:, 0:512], rstd2[:, t:t + 1], fing[:, 0:512],
                                                                ALU.mult, ALU.mult),
                        [otk, tok(f"rstd2_{t}"), tok("fing")], [otk])
                    dma(out_d[rows, 0:512], ob[:, 0:512], [otk], [tok(f"outd{j}")], f"ostc{j}")
                    vop("dve", lambda e: e.scalar_tensor_tensor(ob[:, 512:1024], ob[:, 512:1024], rstd2[:, t:t + 1],
                                                                fing[:, 512:1024], ALU.mult, ALU.mult),
                        [tok("lastH1"), tok(f"rstd2_{t}"), tok("fing")], [tok("lastH1")])
                    sch.op("pool", lambda e: e.dma_start(out=out_d[rows, 512:1024], in_=ob[:, 512:1024]),
                           reads=[tok("lastH1")], writes=[tok("outdL")], chan="ostcL")
                if fin_q:
                    fin_q.pop(0)()
                fin_q.append(fin_last)
                continue
            act(th[0][:].bitcast(BF16), ob, AF.Square, [otk], [tok("th0"), tok(f"ss2_{t}")], accum=ss2[:, t:t + 1])
            act(tmp2[:, t:t + 1], ss2[:, t:t + 1], AF.Ln, [tok(f"ss2_{t}"), tok("epsc")], [tok(f"tmp2_{t}")], scale=1.0 / D, bias=epsc[:, 0:1])
            act(rstd2[:, t:t + 1], tmp2[:, t:t + 1], AF.Exp, [tok(f"tmp2_{t}")], [tok(f"rstd2_{t}")], scale=-0.5)

            def fin_tile(t=t, ob=ob, otk=otk, rows=rows, j=j):
                vop("dve", lambda e: e.scalar_tensor_tensor(ob, ob, rstd2[:, t:t + 1], fing[:], ALU.mult, ALU.mult),
                    [otk, tok(f"rstd2_{t}"), tok("fing")], [otk])
                dma(out_d[rows, :], ob, [otk], [tok(f"outd{j}")], f"ostc{j}")
            if fin_q:
                fin_q.pop(0)()
            fin_q.append(fin_tile)
        while fin_q:
            fin_q.pop(0)()


    PRE_C = [0, 1, 2, 3, 8, 9, 10, 11, 12]
    POST_C = [4, 5, 6, 7, 13, 14, 15, 16]
    try:
        f0 = [front_tile_unit(0, j, 1) for j in range(4)]
        sec_inproj_pre(0)
        run_pipeline(f0[:2])
        stage_half0(range(0, 4))
        run_pipeline(f0[2:])
        stage_half0(range(4, 8))
        stage_half1_dmas()
        act(sinkexp[:], sinkraw[:], AF.Exp, [tok("sinkraw")], [tok("sinkexp")])
        for i in range(4):
            vop("dve", (lambda i: lambda e: e.tensor_scalar(sinkbc[:, i, :], neghalf[:, 0:128], 0.0,
                                                            sinkexp[:, i:i + 1], ALU.mult, ALU.add))(i),
                [tok("neghalf"), tok("sinkexp")], [tok("sinkbc")])
        kbanks = inproj_kouter([0, 1, 2, 3])
        for c in range(17):
            sec_inproj_chunks(0, [c], pre_banks=kbanks)
            if c < 8:
                cast_half1(c)
                if c < 4:
                    later_dma(c)
            elif c == 8:
                cast_small()
                dma(fing[:], fing_d[:, :], [], [tok("fing")], "fing")
        for k in range(8):
            sch.op("pool", (lambda k: lambda e: e.dma_start(out=w_out_bf[:, k, :], in_=w_out_d[k * 128:(k + 1) * 128, :]))(k),
                   reads=[], writes=[t_wout[k]], chan=f"wout{k}")
        for tb in range(nblocks):
            if tb > 0:
                sec_inproj_chunks(tb, POST_C)
            sec_inproj_post(tb)
            sec_stats(tb)
            sec_swa(tb)
            sec_upproj(tb)
            sec_mla(tb)
            if tb + 1 < nblocks:
                sec_inproj_pre(tb + 1)
                sec_inproj_chunks(tb + 1, PRE_C)
            sec_outproj(tb)
    except _Stop:
        pass

    if debug:
        dbuf = {"hT": (hT[:].rearrange("p a b -> p (a b)"), [tok("hT")]),
                "gate": (gate[:].rearrange("p a b -> p (a b)"), [tok(f"gate{i}") for i in range(8)]),
                "QnT": (QnT[:].rearrange("p a b -> p (a b)"), [tok("QnT")]),
                "QrT": (QrT[:].rearrange("p a b -> p (a b)"), [tok("QrT")]),
                "QsT": (QsT[:].rearrange("p a b -> p (a b)"), [tok("QsT")]),
                "KnT": (KnT[:].rearrange("p a b -> p (a b)"), [tok("KnT")]),
                "KrT": (KrT[:], [tok("KrT")]),
                "Vm": (Vm[:].rearrange("p a b -> p (a b)"), [tok("Vm")]),
                "cn": (cn_bf[:].rearrange("p a b -> p (a b)"), [tok("cn_bf0"), tok("cn_bf1"), tok("cn_bf2")]),
                "KsT": (KsT[:].rearrange("p a b -> p (a b)"), [tok("KsT")]),
                "Vs": (Vs[:].rearrange("p a b c -> p (a b c)"), [tok("Vs")]),
                "rstd4": (rstd4[:], [tok(f"rstd4_{t}") for t in range(NT)]),
                }
        dst = sb("dbgst", (128, 2048), F32)
        col = 0
        for (name, ncols) in debug["items"]:
            apf, rtoks = dbuf[name]
            for c1 in range(0, ncols, 2048):
                n = min(2048, ncols - c1)
                vop("dve", (lambda apf, c1, n: lambda e: e.tensor_copy(dst[:, 0:n], apf[:, c1:c1 + n]))(apf, c1, n),
                    rtoks, [tok("dbgst")])
                dma(dbg_d[:, col:col + n], dst[:, 0:n], [tok("dbgst")], [tok("dbgd")], "dbgc")
                col += n

    fin_reads = [tok(f"outd{i}") for i in range(4)] + [tok("outdL")] + ([tok("dbgd")] if debug else [])
    sch.op("sp", lambda e: e.nop(), reads=fin_reads, writes=[])

    sch.finalize()

    sems = {e: es.enter_context(nc.semaphore(f"sem_{e}")) for e in ENGS if e != "sp"}
    chan_sems = {c: es.enter_context(nc.semaphore(f"ch_{c}")) for c in sch.chan_cnt}
    with nc.Block() as block:
        @block.sync
        def _(e):
            sch.emit("sp", e, sems, chan_sems)

        @block.scalar
        def _(e):
            sch.emit("act", e, sems, chan_sems)

        @block.vector
        def _(e):
            sch.emit("dve", e, sems, chan_sems)

        @block.gpsimd
        def _(e):
            sch.emit("pool", e, sems, chan_sems)

        @block.tensor
        def _(e):
            sch.emit("pe", e, sems, chan_sems)
    es.close()
    return nc


def _host_consts():
    ident = np.eye(128, dtype=np.float32)
    rm = np.zeros((128, 128), np.float32)
    for do in range(128):
        if do % 64 < 32:
            rm[do + 32, do] = -1.0
        else:
            rm[do - 32, do] = 1.0
    k = np.arange(128)[:, None]
    q = np.arange(128)[None, :]
    maskc = np.where(k <= q, 0.0, NEG).astype(np.float32)
    maskp = np.where(k > q, 0.0, NEG).astype(np.float32)
    cst = np.zeros((128, 1024), np.float32)
    cst[:, 0:128] = ident
    cst[:, 128:256] = rm
    cst[:, 256:384] = maskc
    msw = np.concatenate([np.tile(maskc, (1, 4)), np.tile(maskp, (1, 4))], axis=1).astype(np.float32)
    pos = np.arange(S, dtype=np.float32)
    inv_freq = (1.0 / (np.float32(10000.0) ** (np.arange(0, 64, 2, dtype=np.float32) / np.float32(64)))).astype(np.float32)
    ang = (pos[None, :] * inv_freq[:, None]).astype(np.float32)
    cos32 = np.cos(ang).astype(np.float32)
    sin32 = np.sin(ang).astype(np.float32)
    cosT = np.tile(cos32, (4, 1)).astype(np.float32)
    sinT = np.tile(sin32, (4, 1)).astype(np.float32)
    return cst, msw, np.ascontiguousarray(cosT), np.ascontiguousarray(sinT)


def _layout_inputs(x, ln_mix, w_in, q_a_norm, w_q_up, kv_a_norm, w_kv_up, attn_sinks, w_out, final_norm):
    f = np.float32
    w_in = np.asarray(w_in, f)[0]
    cq = np.arange(0, 256)
    ckv = np.arange(256, 384)
    kr = np.arange(384, 448)
    gm = np.arange(448, 960)
    qs0 = 960
    ks = np.arange(1472, 1600)
    vs = np.arange(1600, 1728)
    gs0 = 1728

    def perm_heads(base):
        cols = []
        for i in range(4):
            cols.append(np.arange(base + i * 64, base + i * 64 + 64))
            cols.append(np.arange(base + (4 + i) * 64, base + (4 + i) * 64 + 64))
        return np.concatenate(cols)

    perm = np.concatenate([cq, ckv, kr, gm, perm_heads(qs0), ks, perm_heads(gs0), vs])
    w_in_p = np.ascontiguousarray(w_in[:, perm])
    wq = np.asarray(w_q_up, f)[0]
    qn = np.concatenate([np.arange(h * 192, h * 192 + 128) for h in range(4)])
    qr = np.concatenate([np.arange(h * 192 + 128, h * 192 + 192) for h in range(4)])
    wq_p = np.ascontiguousarray(wq[:, np.concatenate([qn, qr])])
    wkv = np.asarray(w_kv_up, f)[0]
    kn = np.concatenate([np.arange(h * 256, h * 256 + 128) for h in range(4)])
    vv = np.concatenate([np.arange(h * 256 + 128, h * 256 + 256) for h in range(4)])
    wkv_p = np.ascontiguousarray(wkv[:, np.concatenate([kn, vv])])
    wo = np.asarray(w_out, f)[0]
    rows = [np.arange(0, 512)]
    for i in range(4):
        for g in range(2):
            rows.append(np.arange(512 + (g * 4 + i) * 64, 512 + (g * 4 + i) * 64 + 64))
    wo_p = np.ascontiguousarray(wo[np.concatenate(rows), :])
    lng = np.ascontiguousarray(np.asarray(ln_mix, f)[0].reshape(8, 128).T)
    qag = np.ascontiguousarray(np.asarray(q_a_norm, f)[0].reshape(2, 128).T)
    kvag = np.ascontiguousarray(np.asarray(kv_a_norm, f)[0].reshape(1, 128).T)
    fing = np.ascontiguousarray(np.broadcast_to(np.asarray(final_norm, f)[None, :], (128, D)))
    sk = np.asarray(attn_sinks, f)[0]
    sinkbc = np.ascontiguousarray(np.concatenate([np.broadcast_to(sk[None, 0:4], (64, 4)),
                                                  np.broadcast_to(sk[None, 4:8], (64, 4))], axis=0))
    cst, msw, cosT, sinT = _host_consts()
    shared = {"w_in": w_in_p, "w_qup": wq_p, "w_kvup": wkv_p, "w_out": wo_p, "lng": lng, "qag": qag,
              "kvag": kvag, "fing": fing, "sinkbc": sinkbc, "cosT": cosT, "sinT": sinT, "cst": cst, "msw": msw}
    xs = np.asarray(x, f)
    return [dict(shared, x=np.ascontiguousarray(xs[b])) for b in range(8)]


_NC_CACHE = {}


def kernel(x, ln_mix, w_in, q_a_norm, w_q_up, kv_a_norm, w_kv_up, attn_sinks, w_out, final_norm):
    in_maps = _layout_inputs(x, ln_mix, w_in, q_a_norm, w_q_up, kv_a_norm, w_kv_up, attn_sinks, w_out, final_norm)
    nc = build_nc()
    res = run_bass_kernel_spmd(nc, in_maps, core_ids=list(range(8)))
    return np.stack([np.asarray(r["out"], np.float32) for r in res.results], axis=0)
```

```python
import math
from contextlib import ExitStack

import numpy as np
import concourse.bass as bass
import concourse.mybir as mybir
from concourse.bass_utils import run_bass_kernel_spmd

F32 = mybir.dt.float32
BF16 = mybir.dt.bfloat16
AF = mybir.ActivationFunctionType
ALU = mybir.AluOpType

S = 2048
D = 1024
NT = 16
TB = 512
NTB = 4
EPS = 1e-6
NEG = -30000.0
SC_MLA = 1.0 / math.sqrt(192.0)
SC_SWA = 1.0 / math.sqrt(64.0)
W_IN_COLS = 2304

ENGS = ("pe", "act", "dve", "pool", "sp")


class Tok:
    __slots__ = ("name", "w", "rs", "excl")

    def __init__(self, name):
        self.name = name
        self.w = None
        self.rs = []
        self.excl = False


class Op:
    __slots__ = ("eng", "fn", "kind", "deps", "inc", "cnt", "chan", "val", "idx")

    def __init__(self, eng, fn, kind):
        self.eng = eng
        self.fn = fn
        self.kind = kind
        self.deps = []
        self.inc = False
        self.cnt = None
        self.chan = None
        self.val = None


class Sched:
    def __init__(self):
        self.ops = []
        self.by_eng = {e: [] for e in ENGS}
        self.chan_cnt = {}

    def op(self, eng, fn, reads=(), writes=(), chan=None):
        kind = "d" if chan is not None else "c"
        o = Op(eng, fn, kind)
        o.idx = len(self.ops)
        deps = {}
        ex = [t for t in reads if t.excl and t not in writes]
        if ex:
            writes = list(writes) + ex
        for t in reads:
            if t.w is not None:
                deps[t.w] = "raw"
        for t in writes:
            if t.w is not None and t.w not in deps:
                deps[t.w] = "waw"
            for r in t.rs:
                if r not in deps:
                    deps[r] = "war"
        for d, k in deps.items():
            if d is o:
                continue
            if d.kind == "c" and o.kind == "c" and d.eng == eng and eng == "pe":
                continue
            o.deps.append(d)
            if d.kind == "c":
                d.inc = True
        for t in reads:
            t.rs.append(o)
        for t in writes:
            t.w = o
            t.rs = []
        if kind == "d":
            self.chan_cnt[chan] = self.chan_cnt.get(chan, 0) + 16
            o.chan = chan
            o.val = self.chan_cnt[chan]
        self.ops.append(o)
        self.by_eng[eng].append(o)
        return o

    def finalize(self):
        cnt = {e: 0 for e in ENGS}
        for o in self.ops:
            if o.kind == "c" and o.inc:
                cnt[o.eng] += 1
                o.cnt = cnt[o.eng]

    def emit(self, eng, engobj, sems, chan_sems):
        seen = {}
        for o in self.by_eng[eng]:
            waits = {}
            for d in o.deps:
                if d.kind == "c":
                    key, sem, val = ("e", d.eng), sems[d.eng], d.cnt
                else:
                    key, sem, val = ("c", d.chan), chan_sems[d.chan], d.val
                if key not in waits or waits[key][1] < val:
                    waits[key] = (sem, val)
            for key, (sem, val) in waits.items():
                if seen.get(key, 0) < val:
                    engobj.wait_ge(sem, val)
                    seen[key] = val
            ins = o.fn(engobj)
            if o.kind == "c":
                if o.inc:
                    ins.then_inc(sems[eng], 1)
            else:
                ins.then_inc(chan_sems[o.chan], 16)


class _Stop(Exception):
    pass


def build_nc(debug=None, limit=99, nblocks=NTB, chunks=None, dovs=True):
    nc = bass.Bass("TRN2", target_bir_lowering=False)
    sch = Sched()
    es = ExitStack()

    def dram_in(name, shape):
        return nc.dram_tensor(name, list(shape), F32, kind="ExternalInput").ap()

    x_d = dram_in("x", (S, D))
    w_in_d = dram_in("w_in", (D, 2240))
    w_qup_d = dram_in("w_qup", (256, 768))
    w_kvup_d = dram_in("w_kvup", (128, 1024))
    w_out_d = dram_in("w_out", (D, D))
    lng_d = dram_in("lng", (128, 8))
    qag_d = dram_in("qag", (128, 2))
    kvag_d = dram_in("kvag", (128, 1))
    fing_d = dram_in("fing", (128, D))
    sink_d = dram_in("sinkbc", (128, 4))
    cos_d = dram_in("cosT", (128, S))
    sin_d = dram_in("sinT", (128, S))
    cst_d = dram_in("cst", (128, 1024))
    msw_d = dram_in("msw", (128, 1024))
    out_d = nc.dram_tensor("out", [S, D], F32, kind="ExternalOutput").ap()
    dbg_d = None
    if debug:
        dbg_d = nc.dram_tensor("dbg", [128, debug["cols"]], F32, kind="ExternalOutput").ap()

    def sb(name, shape, dt):
        return es.enter_context(nc.sbuf_tensor(name, list(shape), dt))

    def ps(name, shape, dt):
        return es.enter_context(nc.psum_tensor(name, list(shape), dt))

    w_in_bf = sb("w_in_bf", (128, 8, W_IN_COLS), BF16)
    w_out_bf = sb("w_out_bf", (128, 8, D), BF16)
    w_qup_bf = sb("w_qup_bf", (128, 2, 1024), BF16)
    w_kvup_bf = sb("w_kvup_bf", (128, 1024), BF16)
    wst = [sb(f"wst{i}", (128, 1120), F32) for i in range(2)]
    ident_bf = sb("ident_bf", (128, 128), BF16)
    rm_bf = sb("rm_bf", (128, 128), BF16)
    maskc_bf = sb("maskc_bf", (128, 128), BF16)
    msw_bf = sb("msw_bf", (128, 2, 512), BF16)
    ones_bf = sb("ones_bf", (128, 128), BF16)
    onesg_bf = sb("onesg_bf", (128, 2, 128), BF16)
    cosT = sb("cosT_s", (128, TB), F32)
    sinT = sb("sinT_s", (128, TB), F32)
    lng = sb("lng_s", (128, 8), F32)
    qag = sb("qag_s", (128, 2), F32)
    kvag = sb("kvag_s", (128, 1), F32)
    fing = sb("fing_s", (128, D), F32)
    sinkraw = sb("sinkraw", (128, 4), F32)
    sinkexp = sb("sinkexp", (128, 4), F32)
    sinkbc = sb("sinkbc_s", (128, 4, 128), F32)
    neghalf = sb("neghalf", (128, 128), F32)
    epsc = sb("epsc", (128, 1), F32)

    KnT = sb("KnT", (128, 4, S), BF16)
    KrT = sb("KrT", (128, S), BF16)
    Vm = sb("Vm", (128, NT, 512), BF16)
    KsT = sb("KsT", (128, 2, 640), BF16)
    Vs = sb("Vs", (128, 5, 2, 128), BF16)

    xst = [sb(f"xst{i}", (128, D), F32) for i in range(2)]
    xs_bf = [sb(f"xs_bf{i}", (128, D), BF16) for i in range(2)]
    ssq = sb("ssq", (128, NT), F32)
    tmp4 = sb("tmp4", (128, NT), F32)
    rstd4 = sb("rstd4", (128, NT), F32)
    hT = sb("hT", (128, 8, TB), BF16)
    hT2 = sb("hT2", (128, 8, TB), BF16)
    cq_f = sb("cq_f", (128, 3, TB), F32)
    sq_bf = sb("sq_bf", (128, 3, TB), BF16)
    tstat = sb("tstat", (128, 2, TB), F32)
    rstat = tstat
    cn_bf = sb("cn_bf", (128, 3, TB), BF16)
    QnT = sb("QnT", (128, 4, TB), BF16)
    QrT = sb("QrT", (128, 4, TB), BF16)
    QsT = sb("QsT", (128, 4, TB), BF16)
    gate = sb("gate", (128, 8, TB), BF16)
    abf = [sb(f"abf{i}", (128, TB), BF16) for i in range(2)]
    t1 = [sb(f"t1_{i}", (128, TB), F32) for i in range(2)]
    t2 = [sb(f"t2_{i}", (128, TB), F32) for i in range(2)]
    th = [sb("th0", (128, TB), F32)]
    PT = [sb(f"PT{i}", (128, TB), BF16) for i in range(4)]
    rr = [sb(f"rr{i}", (128, TB), F32) for i in range(2)]
    on = rr
    yb = [wst[i][:, 0:D] for i in range(2)]
    ss2 = sb("ss2", (128, NT), F32)
    tmp2 = sb("tmp2", (128, NT), F32)
    rstd2 = sb("rstd2", (128, NT), F32)

    psA = [ps(f"psA{i}", (128, 512), F32) for i in range(4)]
    psO = [ps(f"psO{i}", (128, 512), F32) for i in range(2)]
    psL = [ps(f"psL{i}", (128, 512), F32) for i in range(2)]

    class T:
        pass

    tk = {}

    def tok(name):
        if name not in tk:
            tk[name] = Tok(name)
        return tk[name]

    rot = {"A": 0, "abf": 0, "t": 0, "th": 0, "PT": 0, "x": 0, "w": 0, "O": 0, "rr": 0, "y": 0}

    def nxt(key, n):
        v = rot[key]
        rot[key] = (v + 1) % n
        return v

    def dma(out_ap, in_ap, reads, writes, chan):
        sch.op("sp", lambda e: e.dma_start(out=out_ap, in_=in_ap), reads=reads, writes=writes, chan=chan)

    def act(out_ap, in_ap, func, reads, writes, scale=None, accum=None, bias=None):
        def fn(e):
            kw = {}
            if bias is not None:
                kw["bias"] = bias
            if scale is not None:
                kw["scale"] = scale
            if accum is not None:
                kw["accum_out"] = accum
            return e.activation(out_ap, in_ap, func, **kw)
        sch.op("act", fn, reads=reads, writes=writes)

    def vop(eng, fn, reads, writes):
        sch.op(eng, fn, reads=reads, writes=writes)

    def mm_group(mms, reads, writes):
        def fn(e):
            ins = None
            for (o, l, r, st, sp_) in mms:
                ins = e.matmul(o, l, r, start=st, stop=sp_)
            return ins
        sch.op("pe", fn, reads=reads, writes=writes)

    cast_engs = ["act", "dve"]

    def cast_scaled(eng, out_ap, in_ap, scale_ap, reads, writes):
        if eng == "act":
            if scale_ap is None:
                act(out_ap, in_ap, AF.Copy, reads, writes)
            else:
                act(out_ap, in_ap, AF.Identity, reads, writes, scale=scale_ap)
        else:
            if scale_ap is None:
                vop(eng, lambda e: e.tensor_copy(out_ap, in_ap), reads, writes)
            else:
                vop(eng, lambda e: e.tensor_scalar(out_ap, in_ap, scale_ap, None, ALU.mult), reads, writes)

    vm_f = Vm[:].rearrange("p a b -> p (a b)").bitcast(F32)
    kn_f = KnT[:].rearrange("p a b -> p (a b)").bitcast(F32)
    NSL = 8
    wsl = [wst[0][:, :], wst[1][:, :]] + [vm_f[:, i * 1120:(i + 1) * 1120] for i in range(3)] + \
          [kn_f[:, i * 1120:(i + 1) * 1120] for i in range(3)]
    wtok = [tok(f"wst{i}") for i in range(NSL)]

    def stage(dram_ap, ncols):
        sl = nxt("w", NSL)
        dma(wsl[sl][:, 0:ncols], dram_ap, [], [wtok[sl]], f"wst{sl}")
        return sl

    def inherit(dst, srcs):
        for s_ in srcs:
            if s_.w is not None:
                dst.rs.append(s_.w)
            dst.rs.extend(s_.rs)

    x_tok = [tok(f"xst{i}") for i in range(2)]

    def load_x(t):
        sl = nxt("x", 2)
        dma(xst[sl][:], x_d[t * 128:(t + 1) * 128, :], [], [x_tok[sl]], f"xst{sl}")
        return sl

    xslots0 = [load_x(0), load_x(1)]

    sl = stage(cst_d[:, :], 1024)
    vop("dve", (lambda sl: lambda e: e.tensor_copy(ident_bf[:], wsl[sl][:, 0:128]))(sl), [wtok[sl]], [tok("ident")])
    vop("dve", (lambda sl: lambda e: e.tensor_copy(rm_bf[:], wsl[sl][:, 128:256]))(sl), [wtok[sl]], [tok("rm")])
    vop("dve", (lambda sl: lambda e: e.tensor_copy(maskc_bf[:], wsl[sl][:, 256:384]))(sl), [wtok[sl]], [tok("maskc")])
    vop("pool", lambda e: e.memset(ones_bf[:], 1.0), [], [tok("ones")])
    vop("pool", lambda e: e.memset(onesg_bf[:], 0.0), [], [tok("onesg")])
    vop("pool", lambda e: e.memset(onesg_bf[:, 0, 0:64], 1.0), [], [tok("onesg")])
    vop("pool", lambda e: e.memset(onesg_bf[:, 1, 64:128], 1.0), [], [tok("onesg")])
    vop("pool", lambda e: e.memset(neghalf[:], -0.5), [], [tok("neghalf")])
    vop("pool", lambda e: e.memset(epsc[:], EPS), [], [tok("epsc")])
    vop("pool", lambda e: e.memset(KsT[:], 0.0), [], [tok("KsT")])
    vop("pool", lambda e: e.memset(Vs[:], 0.0), [], [tok("Vs")])
    vop("pool", lambda e: e.memset(w_in_bf[:, :, 448:512], 0.0), [], [tok("w_in_pad")])
    vop("pool", lambda e: e.memset(w_qup_bf[:, :, 512:1024], 0.0), [], [tok("w_qup_pad")])

    dma(lng[:], lng_d[:, :], [], [tok("lng")], "lng")
    dma(qag[:], qag_d[:, :], [], [tok("qag")], "qag")
    dma(kvag[:], kvag_d[:, :], [], [tok("kvag")], "kvag")
    dma(sinkraw[:], sink_d[:, :], [], [tok("sinkraw")], "sink")

    t_wq = tok("w_qup_bf")
    t_wkv = tok("w_kvup_bf")
    wout_slots = []

    t_wh0 = [tok(f"w_in_h0_{k}{p}") for k in range(8) for p in "ab"]
    t_wh1 = [tok(f"w_in_h1_{k}{p}") for k in range(8) for p in "ab"]
    h1_slots = {}
    small_slots = {}

    def stage_half0(ks):
        ci = 0
        for k in ks:
            sl = stage(w_in_d[k * 128:(k + 1) * 128, 0:1120], 1120)
            g = lng[:, k:k + 1]
            rd = [wtok[sl], tok("lng"), tok("w_in_pad")]
            cast_scaled(cast_engs[ci % 2], w_in_bf[:, k, 0:448], wsl[sl][:, 0:448], g, rd, [tok(f"w_in_h0_{k}a")])
            ci += 1
            cast_scaled(cast_engs[ci % 2], w_in_bf[:, k, 512:1184], wsl[sl][:, 448:1120], g, rd, [tok(f"w_in_h0_{k}b")])
            ci += 1

    def stage_half1_dmas():
        for k in range(8):
            h1_slots[k] = stage(w_in_d[k * 128:(k + 1) * 128, 1120:2240], 1120)

    def cast_half1(k):
        sl = h1_slots[k]
        g = lng[:, k:k + 1]
        rd = [wtok[sl], tok("lng")]
        cast_scaled("act", w_in_bf[:, k, 1184:1744], wsl[sl][:, 0:560], g, rd, [tok(f"w_in_h1_{k}a")])
        cast_scaled("dve", w_in_bf[:, k, 1744:2304], wsl[sl][:, 560:1120], g, rd, [tok(f"w_in_h1_{k}b")])

    def later_dma(i):
        if i == 0:
            small_slots["q0"] = stage(w_qup_d[0:128, :], 768)
        elif i == 1:
            small_slots["q1"] = stage(w_qup_d[128:256, :], 768)
        elif i == 2:
            small_slots["kv"] = stage(w_kvup_d[:, :], 1024)
        elif i == 3:
            small_slots["msw"] = stage(msw_d[:, :], 1024)
        else:
            k = i - 4
            wout_slots.append(stage(w_out_d[k * 128:(k + 1) * 128, :], 1024))

    def cast_small():
        for k in range(2):
            sl = small_slots[f"q{k}"]
            cast_scaled("dve", w_qup_bf[:, k, 0:512], wsl[sl][:, 0:512], qag[:, k:k + 1],
                        [wtok[sl], tok("qag"), tok("w_qup_pad")], [t_wq])
            dst = w_qup_bf[:, k, 512:1024].rearrange("p (h c) -> p h c", c=128)[:, :, 0:64]
            srcv = wsl[sl][:, 512:768].rearrange("p (h c) -> p h c", c=64)
            cast_scaled("dve", dst, srcv, qag[:, k:k + 1], [wtok[sl], tok("qag"), tok("w_qup_pad")], [t_wq])
        sl = small_slots["kv"]
        cast_scaled("act", w_kvup_bf[:, :], wsl[sl][:, 0:1024], kvag[:, 0:1], [wtok[sl], tok("kvag")], [t_wkv])
        sl = small_slots["msw"]
        cast_scaled("dve", msw_bf[:].rearrange("p a b -> p (a b)"), wsl[sl][:, 0:1024], None, [wtok[sl]], [tok("msw")])

    t_wout = [tok(f"w_out_bf{k}") for k in range(8)]
    ytok = [tok("yb0"), tok("yb1")]


    tA = [tok(f"psA{i}") for i in range(4)]
    tO = [tok(f"psO{i}") for i in range(2)]
    tL = [tok(f"psL{i}") for i in range(2)]
    for t_ in tA + tO + tL:
        t_.excl = True

    evac_rr = [0]

    def evac_eng():
        evac_rr[0] = (evac_rr[0] + 1) % 3
        return "dve" if evac_rr[0] == 0 else "act"

    def copy_evac(eng, out_ap, in_ap, reads, writes):
        if eng == "act":
            act(out_ap, in_ap, AF.Copy, reads, writes)
        else:
            vop(eng, lambda e: e.tensor_copy(out_ap, in_ap), reads, writes)

    def rope_from_bank(a, dests, writes):
        sa = nxt("abf", 2)
        act(abf[sa][:], psA[a][:], AF.Copy, [tA[a]], [tok(f"abf{sa}")])

        def part2():
            b = nxt("A", 4)
            mm_group([(psA[b][:], rm_bf[:], abf[sa][:], True, True)], [tok("rm"), tok(f"abf{sa}")], [tA[b]])
            st = nxt("t", 2)
            vop("dve", lambda e: e.tensor_tensor(t1[st][:], psA[a][:], cosT[:], ALU.mult),
                [tA[a], tok("cos")], [tok(f"t1_{st}")])
            vop("dve", lambda e: e.tensor_tensor(t2[st][:], psA[b][:], sinT[:], ALU.mult),
                [tA[b], tok("sin")], [tok(f"t2_{st}")])
            for (dap, psl) in dests:
                vop("pool", (lambda dap, psl: lambda e: e.tensor_tensor(dap, t1[st][psl, :], t2[st][psl, :], ALU.add))(dap, psl),
                    [tok(f"t1_{st}"), tok(f"t2_{st}")], writes)
        return part2

    pend = []

    def flush_pending():
        while pend:
            pend.pop(0)()

    defer = []

    def tick_defer(force=False):
        keep = []
        for item in defer:
            item[0] -= 1
            if force or item[0] <= 0:
                item[1]()
            else:
                keep.append(item)
        defer[:] = keep

    def run_pipeline(units, lag=2):
        n = len(units)
        due = {}
        for i in range(n):
            units[i]["qk"](units[i])
            due.setdefault(i + units[i].get("lag", lag), []).append(i)
            for j in due.pop(i, []):
                units[j]["pv"](units[j])
        for k in sorted(due):
            for j in due[k]:
                units[j]["pv"](units[j])

    def recip_act(r, l_ap, reads):
        act(rr[r][:], l_ap, AF.Ln, reads, [tok(f"rr{r}")])
        act(rr[r][:], rr[r][:], AF.Exp, [tok(f"rr{r}")], [tok(f"rr{r}")], scale=-1.0)

    def gate_from_bank(a, dest_ap, writes):
        s_ = nxt("th", 1)
        act(th[s_][:], psA[a][:], AF.Tanh, [tA[a]], [tok(f"th{s_}")], scale=0.5)
        vop("dve", lambda e: e.scalar_tensor_tensor(dest_ap, th[s_][:], 1.0, psA[a][:], ALU.add, ALU.mult),
            [tok(f"th{s_}"), tA[a]], writes)

    def chk(k):
        if limit < k:
            raise _Stop()

    HT = [hT, hT2]

    def blk_vars(tb):
        c0 = tb * TB
        return c0, slice(c0, c0 + TB), tok(f"hT{tb % 2}"), HT[tb % 2]

    def front_tile_unit(tb, j, lag):
        c0, tcols, t_hT, hT = blk_vars(tb)
        t = 4 * tb + j

        def qk(U):
            if tb == 0 and j < 2:
                sl = xslots0[j]
            else:
                sl = load_x(t)
            U["sl"] = sl
            vop("dve", lambda e: e.scalar_tensor_tensor(xs_bf[sl][:], xst[sl][:], 1.0, xst[sl][:], ALU.mult, ALU.mult,
                                                        accum_out=ssq[:, t:t + 1]),
                [x_tok[sl]], [tok(f"xs_bf{sl}"), tok(f"ssq{t}")])
            act(tmp4[:, t:t + 1], ssq[:, t:t + 1], AF.Ln, [tok(f"ssq{t}"), tok("epsc")], [tok(f"tmp4_{t}")], scale=1.0 / D, bias=epsc[:, 0:1])
            act(rstd4[:, t:t + 1], tmp4[:, t:t + 1], AF.Exp, [tok(f"tmp4_{t}")], [tok(f"rstd4_{t}")], scale=-0.5)
            vop("dve", lambda e: e.tensor_scalar(xs_bf[sl][:], xst[sl][:], rstd4[:, t:t + 1], None, ALU.mult),
                [x_tok[sl], tok(f"rstd4_{t}")], [tok(f"xs_bf{sl}")])

        def pvf(U):
            sl = U["sl"]
            a = nxt("A", 4)
            psT = psA[a][:].bitcast(BF16)

            def tr_fn(e):
                ins = None
                for kc in range(8):
                    ins = e.transpose(psT[:, kc * 128:(kc + 1) * 128], xs_bf[sl][:, kc * 128:(kc + 1) * 128], ident_bf[:])
                return ins
            sch.op("pe", tr_fn, reads=[tok(f"xs_bf{sl}"), tok("ident")], writes=[tA[a]])
            copy_evac("act", hT[:, :, j * 128:(j + 1) * 128], psT.rearrange("p (k c) -> p k c", c=128),
                      [tA[a]], [t_hT])
        return {"qk": qk, "pv": pvf, "lag": lag}

    def sec_front(tb):
        run_pipeline([front_tile_unit(tb, j, 1) for j in range(4)])

    def sec_inproj_pre(tb):
        c0, tcols, t_hT, hT = blk_vars(tb)
        dma(cosT[:], cos_d[:, tcols], [], [tok("cos")], "cos")
        dma(sinT[:], sin_d[:, tcols], [], [tok("sin")], "sin")
        if tb > 0:
            vop("pool", lambda e: e.tensor_copy(KsT[:, :, 0:128], KsT[:, :, 512:640]), [tok("KsT")], [tok("KsT")])
            vop("pool", lambda e: e.tensor_copy(Vs[:, 0, :, :], Vs[:, 4, :, :]), [tok("Vs")], [tok("Vs")])

    def inproj_kouter(cs):
        c0, tcols, t_hT, hT = blk_vars(0)
        banks = {c: nxt("A", 4) for c in cs}
        for kc in range(8):
            for c in cs:
                a = banks[c]
                mm_group([(psA[a][:], w_in_bf[:, kc, c * 128:(c + 1) * 128], hT[:, kc, :], kc == 0, kc == 7)],
                         [t_hT, tok(f"w_in_h0_{kc}a")], [tA[a]])
        return banks

    def sec_inproj_chunks(tb, cs, pre_banks=None):
        c0, tcols, t_hT, hT = blk_vars(tb)

        def inproj_chunk(c):
            if pre_banks is not None and c in pre_banks:
                return pre_banks[c]
            a = nxt("A", 4)
            mm_group([(psA[a][:], w_in_bf[:, kc, c * 128:(c + 1) * 128], hT[:, kc, :], kc == 0, kc == 7)
                      for kc in range(8)], [t_hT] + t_wh0 + (t_wh1 if c > 8 else []), [tA[a]])
            return a

        for c in cs:
            a = inproj_chunk(c)
            flush_pending()
            if c <= 2:
                act(sq_bf[:, c, :], psA[a][:], AF.Square, [tA[a]], [tok(f"sq_bf{c}")])
                vop("dve", (lambda a, c: lambda e: e.tensor_copy(cq_f[:, c, :], psA[a][:]))(a, c), [tA[a]], [tok(f"cq_f{c}")])
            elif c == 3:
                pend.append(rope_from_bank(a, [(KrT[:, tcols], slice(0, 128))], [tok("KrT")]))
            elif c <= 7:
                gate_from_bank(a, gate[:, c - 4, :], [tok(f"gate{c - 4}")])
            elif c <= 11:
                pend.append(rope_from_bank(a, [(QsT[:, c - 8, :], slice(0, 128))], [tok("QsT")]))
            elif c == 12:
                pend.append(rope_from_bank(a, [(KsT[0:64, 0, 128:640], slice(0, 64)), (KsT[64:128, 1, 128:640], slice(64, 128))],
                                           [tok("KsT")]))
            else:
                gate_from_bank(a, gate[:, 4 + (c - 13), :], [tok(f"gate{4 + c - 13}")])

    def sec_inproj_post(tb):
        c0, tcols, t_hT, hT = blk_vars(tb)
        a = nxt("A", 4)
        mms = []
        for j in range(4):
            for kc in range(8):
                mms.append((psA[a][:, j * 128:(j + 1) * 128], hT[:, kc, j * 128:(j + 1) * 128],
                            w_in_bf[:, kc, 2176:2304], kc == 0, kc == 7))
        mm_group(mms, [t_hT] + t_wh0 + t_wh1, [tA[a]])
        flush_pending()
        pv = psA[a][:].rearrange("p (j c) -> p j c", c=128)
        vop("dve", (lambda pv: lambda e: e.tensor_copy(Vs[:, 1:5, 0, 0:64], pv[:, :, 0:64]))(pv), [tA[a]], [tok("Vs")])
        vop("dve", (lambda pv: lambda e: e.tensor_copy(Vs[:, 1:5, 1, 64:128], pv[:, :, 64:128]))(pv), [tA[a]], [tok("Vs")])

        if tb == 0:
            inherit(ytok[0], [wtok[0]])
            inherit(ytok[1], [wtok[1]])
            inherit(tok("Vm"), wtok[2:5])
            inherit(tok("KnT"), wtok[5:8])

    def sec_stats(tb):
        c0, tcols, t_hT, hT = blk_vars(tb)
        a = nxt("A", 4)
        mm_group([(psA[a][:], ones_bf[:], sq_bf[:, 0, :], True, False),
                  (psA[a][:], ones_bf[:], sq_bf[:, 1, :], False, True)],
                 [tok("ones"), tok("sq_bf0"), tok("sq_bf1")], [tA[a]])
        act(tstat[:, 0, :], psA[a][:], AF.Ln, [tA[a], tok("epsc")], [tok("tstat0")], scale=1.0 / 256.0, bias=epsc[:, 0:1])
        a = nxt("A", 4)
        mm_group([(psA[a][:], ones_bf[:], sq_bf[:, 2, :], True, True)], [tok("ones"), tok("sq_bf2")], [tA[a]])
        act(tstat[:, 1, :], psA[a][:], AF.Ln, [tA[a], tok("epsc")], [tok("tstat1")], scale=1.0 / 128.0, bias=epsc[:, 0:1])
        for q in range(2):
            act(rstat[:, q, :], tstat[:, q, :], AF.Exp, [tok(f"tstat{q}")], [tok(f"tstat{q}")], scale=-0.5)
        for c in range(3):
            q = 0 if c < 2 else 1
            vop("dve", (lambda c, q: lambda e: e.tensor_tensor(cn_bf[:, c, :], cq_f[:, c, :], rstat[:, q, :], ALU.mult))(c, q),
                [tok(f"cq_f{c}"), tok(f"tstat{q}")], [tok(f"cn_bf{c}")])


    def sec_swa(tb):
        c0, tcols, t_hT, hT = blk_vars(tb)
        swa_units = []
        for jq in range(4):
            qt = 4 * tb + jq
            grp = {"o": None}
            ul = [(g, ks) for ks in (jq, jq + 1) if not (qt == 0 and ks == 0) for g in (0, 1)]
            qcols = slice(jq * 128, (jq + 1) * 128)
            for u, (g, ks) in enumerate(ul):
                def qk(U, g=g, ks=ks, jq=jq, qcols=qcols):
                    a = nxt("A", 4)
                    mi = 0 if ks == jq + 1 else 1
                    mm_group([(psA[a][:], KsT[:, g, ks * 128:(ks + 1) * 128], QsT[:, :, qcols], True, False),
                              (psA[a][:], ident_bf[:], msw_bf[:, mi, :], False, True)],
                             [tok("KsT"), tok("QsT"), tok("ident"), tok("msw")], [tA[a]])
                    p = nxt("PT", 4)
                    act(PT[p][:], psA[a][:], AF.Exp, [tA[a]], [tok(f"PT{p}")], scale=SC_SWA)
                    U["p"] = p

                def pvf(U, g=g, ks=ks, first=(u == 0), last=(u == len(ul) - 1), grp=grp, qcols=qcols):
                    tick_defer()
                    if first:
                        grp["o"] = nxt("O", 2)
                    o = grp["o"]
                    p = U["p"]
                    mm_group([(psO[o][:], Vs[:, ks, g, :], PT[p][:], first, last),
                              (psL[o][:], onesg_bf[:, g, :], PT[p][:], first, last)],
                             [tok("Vs"), tok("onesg"), tok(f"PT{p}")], [tO[o], tL[o]])
                    if last:
                        def fin():
                            r = nxt("rr", 2)
                            vop("dve", lambda e: e.tensor_tensor(rr[r][:], psL[o][:], sinkbc[:].rearrange("p a b -> p (a b)"), ALU.add),
                                [tL[o], tok("sinkbc")], [tok(f"rr{r}")])
                            recip_act(r, rr[r][:], [tok(f"rr{r}")])
                            vop("dve", lambda e: e.scalar_tensor_tensor(rr[r][:], psO[o][:], 0.5, rr[r][:], ALU.mult, ALU.mult),
                                [tO[o], tok(f"rr{r}")], [tok(f"rr{r}")])
                            gts = [tok(f"gate{4 + i}") for i in range(4)]
                            vop("pool", lambda e: e.tensor_tensor(gate[:, 4:8, qcols], rr[r][:].rearrange("p (i q) -> p i q", q=128),
                                                                  gate[:, 4:8, qcols], ALU.mult),
                                [tok(f"rr{r}")] + gts, gts)
                        defer.append([1, fin])
                swa_units.append({"qk": qk, "pv": pvf})
        run_pipeline(swa_units, lag=3)
        tick_defer(force=True)


    def keep_warm(n=1):
        od = rot["O"]
        mm_group([(psL[od][:], ones_bf[:], w_kvup_bf[:, 0:512], True, True)] * n, [t_wkv, tok("ones")], [tL[od]])

    def sec_upproj(tb):
        c0, tcols, t_hT, hT = blk_vars(tb)
        for ch in range(8):
            a = nxt("A", 4)
            mm_group([(psA[a][:], w_qup_bf[:, kc, ch * 128:(ch + 1) * 128], cn_bf[:, kc, :], kc == 0, kc == 1)
                      for kc in range(2)], [t_wq, tok("cn_bf0"), tok("cn_bf1")], [tA[a]])
            flush_pending()
            keep_warm()
            if ch < 4:
                copy_evac(evac_eng(), QnT[:, ch, :], psA[a][:], [tA[a]], [tok("QnT")])
            else:
                pend.append(rope_from_bank(a, [(QrT[:, ch - 4, :], slice(0, 128))], [tok("QrT")]))
        for h in range(4):
            a = nxt("A", 4)
            mm_group([(psA[a][:], w_kvup_bf[:, h * 128:(h + 1) * 128], cn_bf[:, 2, :], True, True)],
                     [t_wkv, tok("cn_bf2")], [tA[a]])
            flush_pending()
            keep_warm()
            copy_evac(evac_eng(), KnT[:, h, tcols], psA[a][:], [tA[a]], [tok("KnT")])
        for j in range(4):
            a = nxt("A", 4)
            mm_group([(psA[a][:], cn_bf[:, 2, j * 128:(j + 1) * 128], w_kvup_bf[:, 512:1024], True, True)],
                     [t_wkv, tok("cn_bf2")], [tA[a]])
            keep_warm()
            copy_evac(evac_eng(), Vm[:, 4 * tb + j, :], psA[a][:], [tA[a]], [tok("Vm")])


    def sec_mla(tb):
        c0, tcols, t_hT, hT = blk_vars(tb)
        nk = 4 * tb + 4
        mla_units = []
        for h in range(4):
            grp = {"o": None}
            for kt in range(nk):
                def qk(U, h=h, kt=kt, tb=tb):
                    j = kt - 4 * tb
                    qlo = max(0, j) * 128
                    a = nxt("A", 4)
                    kcols = slice(kt * 128, (kt + 1) * 128)
                    mms = [(psA[a][:, qlo:512], KnT[:, h, kcols], QnT[:, h, qlo:512], True, False),
                           (psA[a][:, qlo:512], KrT[:, kcols], QrT[:, h, qlo:512], False, j < 0)]
                    if j >= 0:
                        mms.append((psA[a][:, qlo:qlo + 128], ident_bf[:], maskc_bf[:], False, True))
                    mm_group(mms, [tok("KnT"), tok("KrT"), tok("QnT"), tok("QrT"), tok("ident"), tok("maskc")], [tA[a]])
                    p = nxt("PT", 4)
                    act(PT[p][:, qlo:512], psA[a][:, qlo:512], AF.Exp, [tA[a]], [tok(f"PT{p}")], scale=SC_MLA)
                    U["p"] = p
                    U["qlo"] = qlo

                def pvf(U, h=h, kt=kt, first=(kt == 0), last=(kt == nk - 1), grp=grp):
                    if first:
                        grp["o"] = nxt("O", 2)
                    o = grp["o"]
                    p = U["p"]
                    qlo = U["qlo"]
                    mm_group([(psO[o][:, qlo:512], Vm[:, kt, h * 128:(h + 1) * 128], PT[p][:, qlo:512], first, last),
                              (psL[o][:, qlo:512], ones_bf[:], PT[p][:, qlo:512], first, last)],
                             [tok("Vm"), tok("ones"), tok(f"PT{p}")], [tO[o], tL[o]])
                    if last:
                        r = nxt("rr", 2)
                        recip_act(r, psL[o][:], [tL[o]])
                        vop("dve", lambda e: e.scalar_tensor_tensor(rr[r][:], psO[o][:], 0.5, rr[r][:], ALU.mult, ALU.mult),
                            [tO[o], tok(f"rr{r}")], [tok(f"rr{r}")])
                        vop("pool", lambda e: e.tensor_tensor(gate[:, h, :], rr[r][:], gate[:, h, :], ALU.mult),
                            [tok(f"rr{r}"), tok(f"gate{h}")], [tok(f"gate{h}")])
                mla_units.append({"qk": qk, "pv": pvf})
        if tb + 1 < nblocks:
            n_u = len(mla_units)
            step = max(1, (n_u // 2) // 4) if n_u > 16 else 4
            lagf = 6 if n_u > 16 else 3
            for j in range(4):
                mla_units.insert(min(len(mla_units), j * (step + 1)), front_tile_unit(tb + 1, j, lagf))
        mla_units.insert(min(len(mla_units), 3), {"qk": (lambda U: keep_warm(6)), "pv": (lambda U: None), "lag": 0})
        run_pipeline(mla_units)


    obuf = [(xst[0][:], x_tok[0], "xst0"), (xst[1][:], x_tok[1], "xst1"), (yb[0], ytok[0], "yb0"), (yb[1], ytok[1], "yb1")]

    def sec_outproj(tb):
        c0, tcols, t_hT, hT = blk_vars(tb)
        gall = [tok(f"gate{i}") for i in range(8)]
        fin_q = []
        for j in range(4):
            t = 4 * tb + j
            ob, otk, och = obuf[j]
            dma(ob, x_d[t * 128:(t + 1) * 128, :], [], [otk], och)
        korder = [4, 5, 6, 7, 0, 1, 2, 3]
        pre = {}
        if tb == nblocks - 1:
            gA = [tok(f"gate{i}") for i in korder[:7]]
            for j in range(2):
                for half in range(2):
                    a = nxt("A", 4)
                    hc = slice(half * 512, (half + 1) * 512)
                    mm_group([(psA[a][:], gate[:, kc, j * 128:(j + 1) * 128], w_out_bf[:, kc, hc], i == 0, False)
                              for i, kc in enumerate(korder[:7])], gA + t_wout, [tA[a]])
                    pre[(j, half)] = a
        for j in range(4):
            t = 4 * tb + j
            rows = slice(t * 128, (t + 1) * 128)
            ob, otk, och = obuf[j]
            for half in range(2):
                hc = slice(half * 512, (half + 1) * 512)
                if (j, half) in pre:
                    a = pre[(j, half)]
                    mm_group([(psA[a][:], gate[:, 3, j * 128:(j + 1) * 128], w_out_bf[:, 3, hc], False, True)],
                             [tok("gate3")] + t_wout, [tA[a]])
                else:
                    a = nxt("A", 4)
                    mm_group([(psA[a][:], gate[:, kc, j * 128:(j + 1) * 128], w_out_bf[:, kc, hc], kc == 0, kc == 7)
                              for kc in range(8)], gall + t_wout, [tA[a]])
                vop("dve", (lambda a, ob, hc: lambda e: e.tensor_tensor(ob[:, hc], psA[a][:], ob[:, hc], ALU.add))(a, ob, hc),
                    [tA[a], otk], [otk])
                if half == 0:
                    flush_pending()
            act(th[0][:].bitcast(BF16), ob, AF.Square, [otk], [tok("th0"), tok(f"ss2_{t}")], accum=ss2[:, t:t + 1])
            act(tmp2[:, t:t + 1], ss2[:, t:t + 1], AF.Ln, [tok(f"ss2_{t}"), tok("epsc")], [tok(f"tmp2_{t}")], scale=1.0 / D, bias=epsc[:, 0:1])
            act(rstd2[:, t:t + 1], tmp2[:, t:t + 1], AF.Exp, [tok(f"tmp2_{t}")], [tok(f"rstd2_{t}")], scale=-0.5)

            def fin_tile(t=t, ob=ob, otk=otk, rows=rows, j=j):
                vop("dve", lambda e: e.scalar_tensor_tensor(ob, ob, rstd2[:, t:t + 1], fing[:], ALU.mult, ALU.mult),
                    [otk, tok(f"rstd2_{t}"), tok("fing")], [otk])
                dma(out_d[rows, :], ob, [otk], [tok(f"outd{j}")], f"ostc{j}")
            if fin_q:
                fin_q.pop(0)()
            fin_q.append(fin_tile)
        while fin_q:
            fin_q.pop(0)()


    PRE_C = [0, 1, 2, 3, 8, 9, 10, 11, 12]
    POST_C = [4, 5, 6, 7, 13, 14, 15, 16]
    try:
        f0 = [front_tile_unit(0, j, 1) for j in range(4)]
        sec_inproj_pre(0)
        run_pipeline(f0[:2])
        stage_half0(range(0, 4))
        run_pipeline(f0[2:])
        stage_half0(range(4, 8))
        stage_half1_dmas()
        act(sinkexp[:], sinkraw[:], AF.Exp, [tok("sinkraw")], [tok("sinkexp")])
        for i in range(4):
            vop("dve", (lambda i: lambda e: e.tensor_scalar(sinkbc[:, i, :], neghalf[:, 0:128], 0.0,
                                                            sinkexp[:, i:i + 1], ALU.mult, ALU.add))(i),
                [tok("neghalf"), tok("sinkexp")], [tok("sinkbc")])
        kbanks = inproj_kouter([0, 1, 2, 3])
        for c in range(17):
            sec_inproj_chunks(0, [c], pre_banks=kbanks)
            if c < 8:
                cast_half1(c)
                if c < 4:
                    later_dma(c)
            elif c == 8:
                cast_small()
                dma(fing[:], fing_d[:, :], [], [tok("fing")], "fing")
        for k in range(8):
            sch.op("pool", (lambda k: lambda e: e.dma_start(out=w_out_bf[:, k, :], in_=w_out_d[k * 128:(k + 1) * 128, :]))(k),
                   reads=[], writes=[t_wout[k]], chan=f"wout{k}")
        for tb in range(nblocks):
            if tb > 0:
                sec_inproj_chunks(tb, POST_C)
            sec_inproj_post(tb)
            sec_stats(tb)
            sec_swa(tb)
            sec_upproj(tb)
            sec_mla(tb)
            if tb + 1 < nblocks:
                sec_inproj_pre(tb + 1)
                sec_inproj_chunks(tb + 1, PRE_C)
            sec_outproj(tb)
    except _Stop:
        pass

    if debug:
        dbuf = {"hT": (hT[:].rearrange("p a b -> p (a b)"), [tok("hT")]),
                "gate": (gate[:].rearrange("p a b -> p (a b)"), [tok(f"gate{i}") for i in range(8)]),
                "QnT": (QnT[:].rearrange("p a b -> p (a b)"), [tok("QnT")]),
                "QrT": (QrT[:].rearrange("p a b -> p (a b)"), [tok("QrT")]),
                "QsT": (QsT[:].rearrange("p a b -> p (a b)"), [tok("QsT")]),
                "KnT": (KnT[:].rearrange("p a b -> p (a b)"), [tok("KnT")]),
                "KrT": (KrT[:], [tok("KrT")]),
                "Vm": (Vm[:].rearrange("p a b -> p (a b)"), [tok("Vm")]),
                "cn": (cn_bf[:].rearrange("p a b -> p (a b)"), [tok("cn_bf0"), tok("cn_bf1"), tok("cn_bf2")]),
                "KsT": (KsT[:].rearrange("p a b -> p (a b)"), [tok("KsT")]),
                "Vs": (Vs[:].rearrange("p a b c -> p (a b c)"), [tok("Vs")]),
                "rstd4": (rstd4[:], [tok(f"rstd4_{t}") for t in range(NT)]),
                }
        dst = sb("dbgst", (128, 2048), F32)
        col = 0
        for (name, ncols) in debug["items"]:
            apf, rtoks = dbuf[name]
            for c1 in range(0, ncols, 2048):
                n = min(2048, ncols - c1)
                vop("dve", (lambda apf, c1, n: lambda e: e.tensor_copy(dst[:, 0:n], apf[:, c1:c1 + n]))(apf, c1, n),
                    rtoks, [tok("dbgst")])
                dma(dbg_d[:, col:col + n], dst[:, 0:n], [tok("dbgst")], [tok("dbgd")], "dbgc")
                col += n

    fin_reads = [tok(f"outd{i}") for i in range(4)] + ([tok("dbgd")] if debug else [])
    sch.op("sp", lambda e: e.nop(), reads=fin_reads, writes=[])

    sch.finalize()

    sems = {e: es.enter_context(nc.semaphore(f"sem_{e}")) for e in ENGS if e != "sp"}
    chan_sems = {c: es.enter_context(nc.semaphore(f"ch_{c}")) for c in sch.chan_cnt}
    with nc.Block() as block:
        @block.sync
        def _(e):
            sch.emit("sp", e, sems, chan_sems)

        @block.scalar
        def _(e):
            sch.emit("act", e, sems, chan_sems)

        @block.vector
        def _(e):
            sch.emit("dve", e, sems, chan_sems)

        @block.gpsimd
        def _(e):
            sch.emit("pool", e, sems, chan_sems)

        @block.tensor
        def _(e):
            sch.emit("pe", e, sems, chan_sems)
    es.close()
    return nc


def _host_consts():
    ident = np.eye(128, dtype=np.float32)
    rm = np.zeros((128, 128), np.float32)
    for do in range(128):
        if do % 64 < 32:
            rm[do + 32, do] = -1.0
        else:
            rm[do - 32, do] = 1.0
    k = np.arange(128)[:, None]
    q = np.arange(128)[None, :]
    maskc = np.where(k <= q, 0.0, NEG).astype(np.float32)
    maskp = np.where(k > q, 0.0, NEG).astype(np.float32)
    cst = np.zeros((128, 1024), np.float32)
    cst[:, 0:128] = ident
    cst[:, 128:256] = rm
    cst[:, 256:384] = maskc
    msw = np.concatenate([np.tile(maskc, (1, 4)), np.tile(maskp, (1, 4))], axis=1).astype(np.float32)
    pos = np.arange(S, dtype=np.float32)
    inv_freq = (1.0 / (np.float32(10000.0) ** (np.arange(0, 64, 2, dtype=np.float32) / np.float32(64)))).astype(np.float32)
    ang = (pos[None, :] * inv_freq[:, None]).astype(np.float32)
    cos32 = np.cos(ang).astype(np.float32)
    sin32 = np.sin(ang).astype(np.float32)
    cosT = np.tile(cos32, (4, 1)).astype(np.float32)
    sinT = np.tile(sin32, (4, 1)).astype(np.float32)
    return cst, msw, np.ascontiguousarray(cosT), np.ascontiguousarray(sinT)


def _layout_inputs(x, ln_mix, w_in, q_a_norm, w_q_up, kv_a_norm, w_kv_up, attn_sinks, w_out, final_norm):
    f = np.float32
    w_in = np.asarray(w_in, f)[0]
    cq = np.arange(0, 256)
    ckv = np.arange(256, 384)
    kr = np.arange(384, 448)
    gm = np.arange(448, 960)
    qs0 = 960
    ks = np.arange(1472, 1600)
    vs = np.arange(1600, 1728)
    gs0 = 1728

    def perm_heads(base):
        cols = []
        for i in range(4):
            cols.append(np.arange(base + i * 64, base + i * 64 + 64))
            cols.append(np.arange(base + (4 + i) * 64, base + (4 + i) * 64 + 64))
        return np.concatenate(cols)

    perm = np.concatenate([cq, ckv, kr, gm, perm_heads(qs0), ks, perm_heads(gs0), vs])
    w_in_p = np.ascontiguousarray(w_in[:, perm])
    wq = np.asarray(w_q_up, f)[0]
    qn = np.concatenate([np.arange(h * 192, h * 192 + 128) for h in range(4)])
    qr = np.concatenate([np.arange(h * 192 + 128, h * 192 + 192) for h in range(4)])
    wq_p = np.ascontiguousarray(wq[:, np.concatenate([qn, qr])])
    wkv = np.asarray(w_kv_up, f)[0]
    kn = np.concatenate([np.arange(h * 256, h * 256 + 128) for h in range(4)])
    vv = np.concatenate([np.arange(h * 256 + 128, h * 256 + 256) for h in range(4)])
    wkv_p = np.ascontiguousarray(wkv[:, np.concatenate([kn, vv])])
    wo = np.asarray(w_out, f)[0]
    rows = [np.arange(0, 512)]
    for i in range(4):
        for g in range(2):
            rows.append(np.arange(512 + (g * 4 + i) * 64, 512 + (g * 4 + i) * 64 + 64))
    wo_p = np.ascontiguousarray(wo[np.concatenate(rows), :])
    lng = np.ascontiguousarray(np.asarray(ln_mix, f)[0].reshape(8, 128).T)
    qag = np.ascontiguousarray(np.asarray(q_a_norm, f)[0].reshape(2, 128).T)
    kvag = np.ascontiguousarray(np.asarray(kv_a_norm, f)[0].reshape(1, 128).T)
    fing = np.ascontiguousarray(np.broadcast_to(np.asarray(final_norm, f)[None, :], (128, D)))
    sk = np.asarray(attn_sinks, f)[0]
    sinkbc = np.ascontiguousarray(np.concatenate([np.broadcast_to(sk[None, 0:4], (64, 4)),
                                                  np.broadcast_to(sk[None, 4:8], (64, 4))], axis=0))
    cst, msw, cosT, sinT = _host_consts()
    shared = {"w_in": w_in_p, "w_qup": wq_p, "w_kvup": wkv_p, "w_out": wo_p, "lng": lng, "qag": qag,
              "kvag": kvag, "fing": fing, "sinkbc": sinkbc, "cosT": cosT, "sinT": sinT, "cst": cst, "msw": msw}
    xs = np.asarray(x, f)
    return [dict(shared, x=np.ascontiguousarray(xs[b])) for b in range(8)]


_NC_CACHE = {}


def kernel(x, ln_mix, w_in, q_a_norm, w_q_up, kv_a_norm, w_kv_up, attn_sinks, w_out, final_norm):
    in_maps = _layout_inputs(x, ln_mix, w_in, q_a_norm, w_q_up, kv_a_norm, w_kv_up, attn_sinks, w_out, final_norm)
    nc = build_nc()
    res = run_bass_kernel_spmd(nc, in_maps, core_ids=list(range(8)))
    return np.stack([np.asarray(r["out"], np.float32) for r in res.results], axis=0)
```

```python
import math
from contextlib import ExitStack

import numpy as np
import concourse.bass as bass
import concourse.mybir as mybir
from concourse.bass_utils import run_bass_kernel_spmd

F32 = mybir.dt.float32
BF16 = mybir.dt.bfloat16
AF = mybir.ActivationFunctionType
ALU = mybir.AluOpType

S = 2048
D = 1024
NT = 16
TB = 512
NTB = 4
EPS = 1e-6
NEG = -30000.0
SC_MLA = 1.0 / math.sqrt(192.0)
SC_SWA = 1.0 / math.sqrt(64.0)
W_IN_COLS = 2304

ENGS = ("pe", "act", "dve", "pool", "sp")


class Tok:
    __slots__ = ("name", "w", "rs", "excl")

    def __init__(self, name):
        self.name = name
        self.w = None
        self.rs = []
        self.excl = False


class Op:
    __slots__ = ("eng", "fn", "kind", "deps", "inc", "cnt", "chan", "val", "idx")

    def __init__(self, eng, fn, kind):
        self.eng = eng
        self.fn = fn
        self.kind = kind
        self.deps = []
        self.inc = False
        self.cnt = None
        self.chan = None
        self.val = None


class Sched:
    def __init__(self):
        self.ops = []
        self.by_eng = {e: [] for e in ENGS}
        self.chan_cnt = {}

    def op(self, eng, fn, reads=(), writes=(), chan=None):
        kind = "d" if chan is not None else "c"
        o = Op(eng, fn, kind)
        o.idx = len(self.ops)
        deps = {}
        ex = [t for t in reads if t.excl and t not in writes]
        if ex:
            writes = list(writes) + ex
        for t in reads:
            if t.w is not None:
                deps[t.w] = "raw"
        for t in writes:
            if t.w is not None and t.w not in deps:
                deps[t.w] = "waw"
            for r in t.rs:
                if r not in deps:
                    deps[r] = "war"
        for d, k in deps.items():
            if d is o:
                continue
            if d.kind == "c" and o.kind == "c" and d.eng == eng and eng == "pe":
                continue
            o.deps.append(d)
            if d.kind == "c":
                d.inc = True
        for t in reads:
            t.rs.append(o)
        for t in writes:
            t.w = o
            t.rs = []
        if kind == "d":
            self.chan_cnt[chan] = self.chan_cnt.get(chan, 0) + 16
            o.chan = chan
            o.val = self.chan_cnt[chan]
        self.ops.append(o)
        self.by_eng[eng].append(o)
        return o

    def finalize(self):
        cnt = {e: 0 for e in ENGS}
        for o in self.ops:
            if o.kind == "c" and o.inc:
                cnt[o.eng] += 1
                o.cnt = cnt[o.eng]

    def emit(self, eng, engobj, sems, chan_sems):
        seen = {}
        for o in self.by_eng[eng]:
            waits = {}
            for d in o.deps:
                if d.kind == "c":
                    key, sem, val = ("e", d.eng), sems[d.eng], d.cnt
                else:
                    key, sem, val = ("c", d.chan), chan_sems[d.chan], d.val
                if key not in waits or waits[key][1] < val:
                    waits[key] = (sem, val)
            for key, (sem, val) in waits.items():
                if seen.get(key, 0) < val:
                    engobj.wait_ge(sem, val)
                    seen[key] = val
            ins = o.fn(engobj)
            if o.kind == "c":
                if o.inc:
                    ins.then_inc(sems[eng], 1)
            else:
                ins.then_inc(chan_sems[o.chan], 16)


class _Stop(Exception):
    pass


def build_nc(debug=None, limit=99, nblocks=NTB, chunks=None, dovs=True):
    nc = bass.Bass("TRN2", target_bir_lowering=False)
    sch = Sched()
    es = ExitStack()

    def dram_in(name, shape):
        return nc.dram_tensor(name, list(shape), F32, kind="ExternalInput").ap()

    x_d = dram_in("x", (S, D))
    w_in_d = dram_in("w_in", (D, 2240))
    w_qup_d = dram_in("w_qup", (256, 768))
    w_kvup_d = dram_in("w_kvup", (128, 1024))
    w_out_d = dram_in("w_out", (D, D))
    lng_d = dram_in("lng", (128, 8))
    qag_d = dram_in("qag", (128, 2))
    kvag_d = dram_in("kvag", (128, 1))
    fing_d = dram_in("fing", (128, D))
    sink_d = dram_in("sinkbc", (128, 4))
    cos_d = dram_in("cosT", (128, S))
    sin_d = dram_in("sinT", (128, S))
    cst_d = dram_in("cst", (128, 1024))
    msw_d = dram_in("msw", (128, 1024))
    out_d = nc.dram_tensor("out", [S, D], F32, kind="ExternalOutput").ap()
    dbg_d = None
    if debug:
        dbg_d = nc.dram_tensor("dbg", [128, debug["cols"]], F32, kind="ExternalOutput").ap()

    def sb(name, shape, dt):
        return es.enter_context(nc.sbuf_tensor(name, list(shape), dt))

    def ps(name, shape, dt):
        return es.enter_context(nc.psum_tensor(name, list(shape), dt))

    w_in_bf = sb("w_in_bf", (128, 8, W_IN_COLS), BF16)
    w_out_bf = sb("w_out_bf", (128, 8, D), BF16)
    w_qup_bf = sb("w_qup_bf", (128, 2, 1024), BF16)
    w_kvup_bf = sb("w_kvup_bf", (128, 1024), BF16)
    wst = [sb(f"wst{i}", (128, 1120), F32) for i in range(2)]
    ident_bf = sb("ident_bf", (128, 128), BF16)
    rm_bf = sb("rm_bf", (128, 128), BF16)
    maskc_bf = sb("maskc_bf", (128, 128), BF16)
    msw_bf = sb("msw_bf", (128, 2, 512), BF16)
    ones_bf = sb("ones_bf", (128, 128), BF16)
    onesg_bf = sb("onesg_bf", (128, 2, 128), BF16)
    cosT = sb("cosT_s", (128, TB), F32)
    sinT = sb("sinT_s", (128, TB), F32)
    lng = sb("lng_s", (128, 8), F32)
    qag = sb("qag_s", (128, 2), F32)
    kvag = sb("kvag_s", (128, 1), F32)
    fing = sb("fing_s", (128, D), F32)
    sinkraw = sb("sinkraw", (128, 4), F32)
    sinkexp = sb("sinkexp", (128, 4), F32)
    sinkbc = sb("sinkbc_s", (128, 4, 128), F32)
    neghalf = sb("neghalf", (128, 128), F32)
    epsc = sb("epsc", (128, 1), F32)

    KnT = sb("KnT", (128, 4, S), BF16)
    KrT = sb("KrT", (128, S), BF16)
    Vm = sb("Vm", (128, NT, 512), BF16)
    KsT = sb("KsT", (128, 2, 640), BF16)
    Vs = sb("Vs", (128, 5, 2, 128), BF16)

    xst = [sb(f"xst{i}", (128, D), F32) for i in range(2)]
    xs_bf = [sb(f"xs_bf{i}", (128, D), BF16) for i in range(2)]
    ssq = sb("ssq", (128, NT), F32)
    tmp4 = sb("tmp4", (128, NT), F32)
    rstd4 = sb("rstd4", (128, NT), F32)
    hT = sb("hT", (128, 8, TB), BF16)
    hT2 = sb("hT2", (128, 8, TB), BF16)
    cq_f = sb("cq_f", (128, 3, TB), F32)
    sq_bf = sb("sq_bf", (128, 3, TB), BF16)
    tstat = sb("tstat", (128, 2, TB), F32)
    rstat = tstat
    cn_bf = sb("cn_bf", (128, 3, TB), BF16)
    QnT = sb("QnT", (128, 4, TB), BF16)
    QrT = sb("QrT", (128, 4, TB), BF16)
    QsT = sb("QsT", (128, 4, TB), BF16)
    gate = sb("gate", (128, 8, TB), BF16)
    abf = [sb(f"abf{i}", (128, TB), BF16) for i in range(2)]
    t1 = [sb(f"t1_{i}", (128, TB), F32) for i in range(2)]
    t2 = [sb(f"t2_{i}", (128, TB), F32) for i in range(2)]
    th = [sb("th0", (128, TB), F32)]
    PT = [sb(f"PT{i}", (128, TB), BF16) for i in range(4)]
    rr = [sb(f"rr{i}", (128, TB), F32) for i in range(2)]
    on = rr
    yb = [wst[i][:, 0:D] for i in range(2)]
    ss2 = sb("ss2", (128, NT), F32)
    tmp2 = sb("tmp2", (128, NT), F32)
    rstd2 = sb("rstd2", (128, NT), F32)

    psA = [ps(f"psA{i}", (128, 512), F32) for i in range(4)]
    psO = [ps(f"psO{i}", (128, 512), F32) for i in range(2)]
    psL = [ps(f"psL{i}", (128, 512), F32) for i in range(2)]

    class T:
        pass

    tk = {}

    def tok(name):
        if name not in tk:
            tk[name] = Tok(name)
        return tk[name]

    rot = {"A": 0, "abf": 0, "t": 0, "th": 0, "PT": 0, "x": 0, "w": 0, "O": 0, "rr": 0, "y": 0}

    def nxt(key, n):
        v = rot[key]
        rot[key] = (v + 1) % n
        return v

    def dma(out_ap, in_ap, reads, writes, chan):
        sch.op("sp", lambda e: e.dma_start(out=out_ap, in_=in_ap), reads=reads, writes=writes, chan=chan)

    def act(out_ap, in_ap, func, reads, writes, scale=None, accum=None, bias=None):
        def fn(e):
            kw = {}
            if bias is not None:
                kw["bias"] = bias
            if scale is not None:
                kw["scale"] = scale
            if accum is not None:
                kw["accum_out"] = accum
            return e.activation(out_ap, in_ap, func, **kw)
        sch.op("act", fn, reads=reads, writes=writes)

    def vop(eng, fn, reads, writes):
        sch.op(eng, fn, reads=reads, writes=writes)

    def mm_group(mms, reads, writes):
        def fn(e):
            ins = None
            for (o, l, r, st, sp_) in mms:
                ins = e.matmul(o, l, r, start=st, stop=sp_)
            return ins
        sch.op("pe", fn, reads=reads, writes=writes)

    cast_engs = ["act", "dve"]

    def cast_scaled(eng, out_ap, in_ap, scale_ap, reads, writes):
        if eng == "act":
            if scale_ap is None:
                act(out_ap, in_ap, AF.Copy, reads, writes)
            else:
                act(out_ap, in_ap, AF.Identity, reads, writes, scale=scale_ap)
        else:
            if scale_ap is None:
                vop(eng, lambda e: e.tensor_copy(out_ap, in_ap), reads, writes)
            else:
                vop(eng, lambda e: e.tensor_scalar(out_ap, in_ap, scale_ap, None, ALU.mult), reads, writes)

    vm_f = Vm[:].rearrange("p a b -> p (a b)").bitcast(F32)
    kn_f = KnT[:].rearrange("p a b -> p (a b)").bitcast(F32)
    NSL = 8
    wsl = [wst[0][:, :], wst[1][:, :]] + [vm_f[:, i * 1120:(i + 1) * 1120] for i in range(3)] + \
          [kn_f[:, i * 1120:(i + 1) * 1120] for i in range(3)]
    wtok = [tok(f"wst{i}") for i in range(NSL)]

    def stage(dram_ap, ncols):
        sl = nxt("w", NSL)
        dma(wsl[sl][:, 0:ncols], dram_ap, [], [wtok[sl]], f"wst{sl}")
        return sl

    def inherit(dst, srcs):
        for s_ in srcs:
            if s_.w is not None:
                dst.rs.append(s_.w)
            dst.rs.extend(s_.rs)

    x_tok = [tok(f"xst{i}") for i in range(2)]

    def load_x(t):
        sl = nxt("x", 2)
        dma(xst[sl][:], x_d[t * 128:(t + 1) * 128, :], [], [x_tok[sl]], f"xst{sl}")
        return sl

    xslots0 = [load_x(0), load_x(1)]

    sl = stage(cst_d[:, :], 1024)
    vop("dve", (lambda sl: lambda e: e.tensor_copy(ident_bf[:], wsl[sl][:, 0:128]))(sl), [wtok[sl]], [tok("ident")])
    vop("dve", (lambda sl: lambda e: e.tensor_copy(rm_bf[:], wsl[sl][:, 128:256]))(sl), [wtok[sl]], [tok("rm")])
    vop("dve", (lambda sl: lambda e: e.tensor_copy(maskc_bf[:], wsl[sl][:, 256:384]))(sl), [wtok[sl]], [tok("maskc")])
    vop("pool", lambda e: e.memset(ones_bf[:], 1.0), [], [tok("ones")])
    vop("pool", lambda e: e.memset(onesg_bf[:], 0.0), [], [tok("onesg")])
    vop("pool", lambda e: e.memset(onesg_bf[:, 0, 0:64], 1.0), [], [tok("onesg")])
    vop("pool", lambda e: e.memset(onesg_bf[:, 1, 64:128], 1.0), [], [tok("onesg")])
    vop("pool", lambda e: e.memset(neghalf[:], -0.5), [], [tok("neghalf")])
    vop("pool", lambda e: e.memset(epsc[:], EPS), [], [tok("epsc")])
    vop("pool", lambda e: e.memset(KsT[:], 0.0), [], [tok("KsT")])
    vop("pool", lambda e: e.memset(Vs[:], 0.0), [], [tok("Vs")])
    vop("pool", lambda e: e.memset(w_in_bf[:, :, 448:512], 0.0), [], [tok("w_in_pad")])
    vop("pool", lambda e: e.memset(w_qup_bf[:, :, 512:1024], 0.0), [], [tok("w_qup_pad")])

    dma(lng[:], lng_d[:, :], [], [tok("lng")], "lng")
    dma(qag[:], qag_d[:, :], [], [tok("qag")], "qag")
    dma(kvag[:], kvag_d[:, :], [], [tok("kvag")], "kvag")
    dma(sinkraw[:], sink_d[:, :], [], [tok("sinkraw")], "sink")

    t_wq = tok("w_qup_bf")
    t_wkv = tok("w_kvup_bf")
    wout_slots = []

    t_wh0 = [tok(f"w_in_h0_{k}{p}") for k in range(8) for p in "ab"]
    t_wh1 = [tok(f"w_in_h1_{k}{p}") for k in range(8) for p in "ab"]
    h1_slots = {}
    small_slots = {}

    def stage_half0(ks):
        ci = 0
        for k in ks:
            sl = stage(w_in_d[k * 128:(k + 1) * 128, 0:1120], 1120)
            g = lng[:, k:k + 1]
            rd = [wtok[sl], tok("lng"), tok("w_in_pad")]
            cast_scaled(cast_engs[ci % 2], w_in_bf[:, k, 0:448], wsl[sl][:, 0:448], g, rd, [tok(f"w_in_h0_{k}a")])
            ci += 1
            cast_scaled(cast_engs[ci % 2], w_in_bf[:, k, 512:1184], wsl[sl][:, 448:1120], g, rd, [tok(f"w_in_h0_{k}b")])
            ci += 1

    def stage_half1_dmas():
        for k in range(8):
            h1_slots[k] = stage(w_in_d[k * 128:(k + 1) * 128, 1120:2240], 1120)

    def cast_half1(k):
        sl = h1_slots[k]
        g = lng[:, k:k + 1]
        rd = [wtok[sl], tok("lng")]
        cast_scaled("act", w_in_bf[:, k, 1184:1744], wsl[sl][:, 0:560], g, rd, [tok(f"w_in_h1_{k}a")])
        cast_scaled("dve", w_in_bf[:, k, 1744:2304], wsl[sl][:, 560:1120], g, rd, [tok(f"w_in_h1_{k}b")])

    def later_dma(i):
        if i == 0:
            small_slots["q0"] = stage(w_qup_d[0:128, :], 768)
        elif i == 1:
            small_slots["q1"] = stage(w_qup_d[128:256, :], 768)
        elif i == 2:
            small_slots["kv"] = stage(w_kvup_d[:, :], 1024)
        elif i == 3:
            small_slots["msw"] = stage(msw_d[:, :], 1024)
        else:
            k = i - 4
            wout_slots.append(stage(w_out_d[k * 128:(k + 1) * 128, :], 1024))

    def cast_small():
        for k in range(2):
            sl = small_slots[f"q{k}"]
            cast_scaled("dve", w_qup_bf[:, k, 0:512], wsl[sl][:, 0:512], qag[:, k:k + 1],
                        [wtok[sl], tok("qag"), tok("w_qup_pad")], [t_wq])
            dst = w_qup_bf[:, k, 512:1024].rearrange("p (h c) -> p h c", c=128)[:, :, 0:64]
            srcv = wsl[sl][:, 512:768].rearrange("p (h c) -> p h c", c=64)
            cast_scaled("dve", dst, srcv, qag[:, k:k + 1], [wtok[sl], tok("qag"), tok("w_qup_pad")], [t_wq])
        sl = small_slots["kv"]
        cast_scaled("act", w_kvup_bf[:, :], wsl[sl][:, 0:1024], kvag[:, 0:1], [wtok[sl], tok("kvag")], [t_wkv])
        sl = small_slots["msw"]
        cast_scaled("dve", msw_bf[:].rearrange("p a b -> p (a b)"), wsl[sl][:, 0:1024], None, [wtok[sl]], [tok("msw")])

    t_wout = [tok(f"w_out_bf{k}") for k in range(8)]
    ytok = [tok("yb0"), tok("yb1")]


    tA = [tok(f"psA{i}") for i in range(4)]
    tO = [tok(f"psO{i}") for i in range(2)]
    tL = [tok(f"psL{i}") for i in range(2)]
    for t_ in tA + tO + tL:
        t_.excl = True

    evac_rr = [0]

    def evac_eng():
        evac_rr[0] = (evac_rr[0] + 1) % 3
        return "dve" if evac_rr[0] == 0 else "act"

    def copy_evac(eng, out_ap, in_ap, reads, writes):
        if eng == "act":
            act(out_ap, in_ap, AF.Copy, reads, writes)
        else:
            vop(eng, lambda e: e.tensor_copy(out_ap, in_ap), reads, writes)

    def rope_from_bank(a, dests, writes):
        sa = nxt("abf", 2)
        act(abf[sa][:], psA[a][:], AF.Copy, [tA[a]], [tok(f"abf{sa}")])

        def part2():
            b = nxt("A", 4)
            mm_group([(psA[b][:], rm_bf[:], abf[sa][:], True, True)], [tok("rm"), tok(f"abf{sa}")], [tA[b]])
            st = nxt("t", 2)
            vop("dve", lambda e: e.tensor_tensor(t1[st][:], psA[a][:], cosT[:], ALU.mult),
                [tA[a], tok("cos")], [tok(f"t1_{st}")])
            vop("dve", lambda e: e.tensor_tensor(t2[st][:], psA[b][:], sinT[:], ALU.mult),
                [tA[b], tok("sin")], [tok(f"t2_{st}")])
            for (dap, psl) in dests:
                vop("pool", (lambda dap, psl: lambda e: e.tensor_tensor(dap, t1[st][psl, :], t2[st][psl, :], ALU.add))(dap, psl),
                    [tok(f"t1_{st}"), tok(f"t2_{st}")], writes)
        return part2

    pend = []

    def flush_pending():
        while pend:
            pend.pop(0)()

    defer = []

    def tick_defer(force=False):
        keep = []
        for item in defer:
            item[0] -= 1
            if force or item[0] <= 0:
                item[1]()
            else:
                keep.append(item)
        defer[:] = keep

    def run_pipeline(units, lag=2):
        n = len(units)
        due = {}
        for i in range(n):
            units[i]["qk"](units[i])
            due.setdefault(i + units[i].get("lag", lag), []).append(i)
            for j in due.pop(i, []):
                units[j]["pv"](units[j])
        for k in sorted(due):
            for j in due[k]:
                units[j]["pv"](units[j])

    def recip_act(r, l_ap, reads):
        act(rr[r][:], l_ap, AF.Ln, reads, [tok(f"rr{r}")])
        act(rr[r][:], rr[r][:], AF.Exp, [tok(f"rr{r}")], [tok(f"rr{r}")], scale=-1.0)

    def gate_from_bank(a, dest_ap, writes):
        s_ = nxt("th", 1)
        act(th[s_][:], psA[a][:], AF.Tanh, [tA[a]], [tok(f"th{s_}")], scale=0.5)
        vop("dve", lambda e: e.scalar_tensor_tensor(dest_ap, th[s_][:], 1.0, psA[a][:], ALU.add, ALU.mult),
            [tok(f"th{s_}"), tA[a]], writes)

    def chk(k):
        if limit < k:
            raise _Stop()

    HT = [hT, hT2]

    def blk_vars(tb):
        c0 = tb * TB
        return c0, slice(c0, c0 + TB), tok(f"hT{tb % 2}"), HT[tb % 2]

    def front_tile_unit(tb, j, lag):
        c0, tcols, t_hT, hT = blk_vars(tb)
        t = 4 * tb + j

        def qk(U):
            if tb == 0 and j < 2:
                sl = xslots0[j]
            else:
                sl = load_x(t)
            U["sl"] = sl
            vop("dve", lambda e: e.scalar_tensor_tensor(xs_bf[sl][:], xst[sl][:], 1.0, xst[sl][:], ALU.mult, ALU.mult,
                                                        accum_out=ssq[:, t:t + 1]),
                [x_tok[sl]], [tok(f"xs_bf{sl}"), tok(f"ssq{t}")])
            act(tmp4[:, t:t + 1], ssq[:, t:t + 1], AF.Ln, [tok(f"ssq{t}"), tok("epsc")], [tok(f"tmp4_{t}")], scale=1.0 / D, bias=epsc[:, 0:1])
            act(rstd4[:, t:t + 1], tmp4[:, t:t + 1], AF.Exp, [tok(f"tmp4_{t}")], [tok(f"rstd4_{t}")], scale=-0.5)
            vop("dve", lambda e: e.tensor_scalar(xs_bf[sl][:], xst[sl][:], rstd4[:, t:t + 1], None, ALU.mult),
                [x_tok[sl], tok(f"rstd4_{t}")], [tok(f"xs_bf{sl}")])

        def pvf(U):
            sl = U["sl"]
            a = nxt("A", 4)
            psT = psA[a][:].bitcast(BF16)

            def tr_fn(e):
                ins = None
                for kc in range(8):
                    ins = e.transpose(psT[:, kc * 128:(kc + 1) * 128], xs_bf[sl][:, kc * 128:(kc + 1) * 128], ident_bf[:])
                return ins
            sch.op("pe", tr_fn, reads=[tok(f"xs_bf{sl}"), tok("ident")], writes=[tA[a]])
            copy_evac("act", hT[:, :, j * 128:(j + 1) * 128], psT.rearrange("p (k c) -> p k c", c=128),
                      [tA[a]], [t_hT])
        return {"qk": qk, "pv": pvf, "lag": lag}

    def sec_front(tb):
        run_pipeline([front_tile_unit(tb, j, 1) for j in range(4)])

    def sec_inproj_pre(tb):
        c0, tcols, t_hT, hT = blk_vars(tb)
        dma(cosT[:], cos_d[:, tcols], [], [tok("cos")], "cos")
        dma(sinT[:], sin_d[:, tcols], [], [tok("sin")], "sin")
        if tb > 0:
            vop("pool", lambda e: e.tensor_copy(KsT[:, :, 0:128], KsT[:, :, 512:640]), [tok("KsT")], [tok("KsT")])
            vop("pool", lambda e: e.tensor_copy(Vs[:, 0, :, :], Vs[:, 4, :, :]), [tok("Vs")], [tok("Vs")])

    def inproj_kouter(cs):
        c0, tcols, t_hT, hT = blk_vars(0)
        banks = {c: nxt("A", 4) for c in cs}
        for kc in range(8):
            for c in cs:
                a = banks[c]
                mm_group([(psA[a][:], w_in_bf[:, kc, c * 128:(c + 1) * 128], hT[:, kc, :], kc == 0, kc == 7)],
                         [t_hT, tok(f"w_in_h0_{kc}a")], [tA[a]])
        return banks

    def sec_inproj_chunks(tb, cs, pre_banks=None):
        c0, tcols, t_hT, hT = blk_vars(tb)

        def inproj_chunk(c):
            if pre_banks is not None and c in pre_banks:
                return pre_banks[c]
            a = nxt("A", 4)
            mm_group([(psA[a][:], w_in_bf[:, kc, c * 128:(c + 1) * 128], hT[:, kc, :], kc == 0, kc == 7)
                      for kc in range(8)], [t_hT] + t_wh0 + (t_wh1 if c > 8 else []), [tA[a]])
            return a

        for c in cs:
            a = inproj_chunk(c)
            flush_pending()
            if c <= 2:
                act(sq_bf[:, c, :], psA[a][:], AF.Square, [tA[a]], [tok(f"sq_bf{c}")])
                vop("dve", (lambda a, c: lambda e: e.tensor_copy(cq_f[:, c, :], psA[a][:]))(a, c), [tA[a]], [tok(f"cq_f{c}")])
            elif c == 3:
                pend.append(rope_from_bank(a, [(KrT[:, tcols], slice(0, 128))], [tok("KrT")]))
            elif c <= 7:
                gate_from_bank(a, gate[:, c - 4, :], [tok(f"gate{c - 4}")])
            elif c <= 11:
                pend.append(rope_from_bank(a, [(QsT[:, c - 8, :], slice(0, 128))], [tok("QsT")]))
            elif c == 12:
                pend.append(rope_from_bank(a, [(KsT[0:64, 0, 128:640], slice(0, 64)), (KsT[64:128, 1, 128:640], slice(64, 128))],
                                           [tok("KsT")]))
            else:
                gate_from_bank(a, gate[:, 4 + (c - 13), :], [tok(f"gate{4 + c - 13}")])

    def sec_inproj_post(tb):
        c0, tcols, t_hT, hT = blk_vars(tb)
        a = nxt("A", 4)
        mms = []
        for j in range(4):
            for kc in range(8):
                mms.append((psA[a][:, j * 128:(j + 1) * 128], hT[:, kc, j * 128:(j + 1) * 128],
                            w_in_bf[:, kc, 2176:2304], kc == 0, kc == 7))
        mm_group(mms, [t_hT] + t_wh0 + t_wh1, [tA[a]])
        flush_pending()
        pv = psA[a][:].rearrange("p (j c) -> p j c", c=128)
        vop("dve", (lambda pv: lambda e: e.tensor_copy(Vs[:, 1:5, 0, 0:64], pv[:, :, 0:64]))(pv), [tA[a]], [tok("Vs")])
        vop("dve", (lambda pv: lambda e: e.tensor_copy(Vs[:, 1:5, 1, 64:128], pv[:, :, 64:128]))(pv), [tA[a]], [tok("Vs")])

        if tb == 0:
            inherit(ytok[0], [wtok[0]])
            inherit(ytok[1], [wtok[1]])
            inherit(tok("Vm"), wtok[2:5])
            inherit(tok("KnT"), wtok[5:8])

    def sec_stats(tb):
        c0, tcols, t_hT, hT = blk_vars(tb)
        a = nxt("A", 4)
        mm_group([(psA[a][:], ones_bf[:], sq_bf[:, 0, :], True, False),
                  (psA[a][:], ones_bf[:], sq_bf[:, 1, :], False, True)],
                 [tok("ones"), tok("sq_bf0"), tok("sq_bf1")], [tA[a]])
        act(tstat[:, 0, :], psA[a][:], AF.Ln, [tA[a], tok("epsc")], [tok("tstat0")], scale=1.0 / 256.0, bias=epsc[:, 0:1])
        a = nxt("A", 4)
        mm_group([(psA[a][:], ones_bf[:], sq_bf[:, 2, :], True, True)], [tok("ones"), tok("sq_bf2")], [tA[a]])
        act(tstat[:, 1, :], psA[a][:], AF.Ln, [tA[a], tok("epsc")], [tok("tstat1")], scale=1.0 / 128.0, bias=epsc[:, 0:1])
        for q in range(2):
            act(rstat[:, q, :], tstat[:, q, :], AF.Exp, [tok(f"tstat{q}")], [tok(f"tstat{q}")], scale=-0.5)
        for c in range(3):
            q = 0 if c < 2 else 1
            vop("dve", (lambda c, q: lambda e: e.tensor_tensor(cn_bf[:, c, :], cq_f[:, c, :], rstat[:, q, :], ALU.mult))(c, q),
                [tok(f"cq_f{c}"), tok(f"tstat{q}")], [tok(f"cn_bf{c}")])


    def sec_swa(tb):
        c0, tcols, t_hT, hT = blk_vars(tb)
        swa_units = []
        for jq in range(4):
            qt = 4 * tb + jq
            grp = {"o": None}
            ul = [(g, ks) for ks in (jq, jq + 1) if not (qt == 0 and ks == 0) for g in (0, 1)]
            qcols = slice(jq * 128, (jq + 1) * 128)
            for u, (g, ks) in enumerate(ul):
                def qk(U, g=g, ks=ks, jq=jq, qcols=qcols):
                    a = nxt("A", 4)
                    mi = 0 if ks == jq + 1 else 1
                    mm_group([(psA[a][:], KsT[:, g, ks * 128:(ks + 1) * 128], QsT[:, :, qcols], True, False),
                              (psA[a][:], ident_bf[:], msw_bf[:, mi, :], False, True)],
                             [tok("KsT"), tok("QsT"), tok("ident"), tok("msw")], [tA[a]])
                    p = nxt("PT", 4)
                    act(PT[p][:], psA[a][:], AF.Exp, [tA[a]], [tok(f"PT{p}")], scale=SC_SWA)
                    U["p"] = p

                def pvf(U, g=g, ks=ks, first=(u == 0), last=(u == len(ul) - 1), grp=grp, qcols=qcols):
                    tick_defer()
                    if first:
                        grp["o"] = nxt("O", 2)
                    o = grp["o"]
                    p = U["p"]
                    mm_group([(psO[o][:], Vs[:, ks, g, :], PT[p][:], first, last),
                              (psL[o][:], onesg_bf[:, g, :], PT[p][:], first, last)],
                             [tok("Vs"), tok("onesg"), tok(f"PT{p}")], [tO[o], tL[o]])
                    if last:
                        def fin():
                            r = nxt("rr", 2)
                            vop("dve", lambda e: e.tensor_tensor(rr[r][:], psL[o][:], sinkbc[:].rearrange("p a b -> p (a b)"), ALU.add),
                                [tL[o], tok("sinkbc")], [tok(f"rr{r}")])
                            recip_act(r, rr[r][:], [tok(f"rr{r}")])
                            vop("dve", lambda e: e.scalar_tensor_tensor(rr[r][:], psO[o][:], 0.5, rr[r][:], ALU.mult, ALU.mult),
                                [tO[o], tok(f"rr{r}")], [tok(f"rr{r}")])
                            gts = [tok(f"gate{4 + i}") for i in range(4)]
                            vop("pool", lambda e: e.tensor_tensor(gate[:, 4:8, qcols], rr[r][:].rearrange("p (i q) -> p i q", q=128),
                                                                  gate[:, 4:8, qcols], ALU.mult),
                                [tok(f"rr{r}")] + gts, gts)
                        defer.append([1, fin])
                swa_units.append({"qk": qk, "pv": pvf})
        run_pipeline(swa_units, lag=3)
        tick_defer(force=True)


    def keep_warm(n=1):
        od = rot["O"]
        mm_group([(psL[od][:], ones_bf[:], w_kvup_bf[:, 0:512], True, True)] * n, [t_wkv, tok("ones")], [tL[od]])

    def sec_upproj(tb):
        c0, tcols, t_hT, hT = blk_vars(tb)
        for ch in range(8):
            a = nxt("A", 4)
            mm_group([(psA[a][:], w_qup_bf[:, kc, ch * 128:(ch + 1) * 128], cn_bf[:, kc, :], kc == 0, kc == 1)
                      for kc in range(2)], [t_wq, tok("cn_bf0"), tok("cn_bf1")], [tA[a]])
            flush_pending()
            keep_warm()
            if ch < 4:
                copy_evac(evac_eng(), QnT[:, ch, :], psA[a][:], [tA[a]], [tok("QnT")])
            else:
                pend.append(rope_from_bank(a, [(QrT[:, ch - 4, :], slice(0, 128))], [tok("QrT")]))
        for h in range(4):
            a = nxt("A", 4)
            mm_group([(psA[a][:], w_kvup_bf[:, h * 128:(h + 1) * 128], cn_bf[:, 2, :], True, True)],
                     [t_wkv, tok("cn_bf2")], [tA[a]])
            flush_pending()
            keep_warm()
            copy_evac(evac_eng(), KnT[:, h, tcols], psA[a][:], [tA[a]], [tok("KnT")])
        for j in range(4):
            a = nxt("A", 4)
            mm_group([(psA[a][:], cn_bf[:, 2, j * 128:(j + 1) * 128], w_kvup_bf[:, 512:1024], True, True)],
                     [t_wkv, tok("cn_bf2")], [tA[a]])
            keep_warm()
            copy_evac(evac_eng(), Vm[:, 4 * tb + j, :], psA[a][:], [tA[a]], [tok("Vm")])


    def sec_mla(tb):
        c0, tcols, t_hT, hT = blk_vars(tb)
        nk = 4 * tb + 4
        mla_units = []
        for h in range(4):
            grp = {"o": None}
            for kt in range(nk):
                def qk(U, h=h, kt=kt, tb=tb):
                    j = kt - 4 * tb
                    qlo = max(0, j) * 128
                    a = nxt("A", 4)
                    kcols = slice(kt * 128, (kt + 1) * 128)
                    mms = [(psA[a][:, qlo:512], KnT[:, h, kcols], QnT[:, h, qlo:512], True, False),
                           (psA[a][:, qlo:512], KrT[:, kcols], QrT[:, h, qlo:512], False, j < 0)]
                    if j >= 0:
                        mms.append((psA[a][:, qlo:qlo + 128], ident_bf[:], maskc_bf[:], False, True))
                    mm_group(mms, [tok("KnT"), tok("KrT"), tok("QnT"), tok("QrT"), tok("ident"), tok("maskc")], [tA[a]])
                    p = nxt("PT", 4)
                    act(PT[p][:, qlo:512], psA[a][:, qlo:512], AF.Exp, [tA[a]], [tok(f"PT{p}")], scale=SC_MLA)
                    U["p"] = p
                    U["qlo"] = qlo

                def pvf(U, h=h, kt=kt, first=(kt == 0), last=(kt == nk - 1), grp=grp):
                    if first:
                        grp["o"] = nxt("O", 2)
                    o = grp["o"]
                    p = U["p"]
                    qlo = U["qlo"]
                    mm_group([(psO[o][:, qlo:512], Vm[:, kt, h * 128:(h + 1) * 128], PT[p][:, qlo:512], first, last),
                              (psL[o][:, qlo:512], ones_bf[:], PT[p][:, qlo:512], first, last)],
                             [tok("Vm"), tok("ones"), tok(f"PT{p}")], [tO[o], tL[o]])
                    if last:
                        r = nxt("rr", 2)
                        recip_act(r, psL[o][:], [tL[o]])
                        vop("dve", lambda e: e.scalar_tensor_tensor(rr[r][:], psO[o][:], 0.5, rr[r][:], ALU.mult, ALU.mult),
                            [tO[o], tok(f"rr{r}")], [tok(f"rr{r}")])
                        vop("pool", lambda e: e.tensor_tensor(gate[:, h, :], rr[r][:], gate[:, h, :], ALU.mult),
                            [tok(f"rr{r}"), tok(f"gate{h}")], [tok(f"gate{h}")])
                mla_units.append({"qk": qk, "pv": pvf})
        if tb + 1 < nblocks:
            n_u = len(mla_units)
            step = max(1, (n_u // 2) // 4) if n_u > 16 else 4
            lagf = 6 if n_u > 16 else 3
            for j in range(4):
                mla_units.insert(min(len(mla_units), j * (step + 1)), front_tile_unit(tb + 1, j, lagf))
        mla_units.insert(min(len(mla_units), 3), {"qk": (lambda U: keep_warm(6)), "pv": (lambda U: None), "lag": 0})
        run_pipeline(mla_units, lag=3)


    obuf = [(xst[0][:], x_tok[0], "xst0"), (xst[1][:], x_tok[1], "xst1"), (yb[0], ytok[0], "yb0"), (yb[1], ytok[1], "yb1")]

    def sec_outproj(tb):
        c0, tcols, t_hT, hT = blk_vars(tb)
        gall = [tok(f"gate{i}") for i in range(8)]
        fin_q = []
        for j in range(4):
            t = 4 * tb + j
            ob, otk, och = obuf[j]
            dma(ob, x_d[t * 128:(t + 1) * 128, :], [], [otk], och)
        korder = [4, 5, 6, 7, 0, 1, 2, 3]
        pre = {}
        if tb == nblocks - 1:
            gA = [tok(f"gate{i}") for i in korder[:7]]
            for j in range(2):
                for half in range(2):
                    a = nxt("A", 4)
                    hc = slice(half * 512, (half + 1) * 512)
                    mm_group([(psA[a][:], gate[:, kc, j * 128:(j + 1) * 128], w_out_bf[:, kc, hc], i == 0, False)
                              for i, kc in enumerate(korder[:7])], gA + t_wout, [tA[a]])
                    pre[(j, half)] = a
        for j in range(4):
            t = 4 * tb + j
            rows = slice(t * 128, (t + 1) * 128)
            ob, otk, och = obuf[j]
            for half in range(2):
                hc = slice(half * 512, (half + 1) * 512)
                if (j, half) in pre:
                    a = pre[(j, half)]
                    mm_group([(psA[a][:], gate[:, 3, j * 128:(j + 1) * 128], w_out_bf[:, 3, hc], False, True)],
                             [tok("gate3")] + t_wout, [tA[a]])
                else:
                    a = nxt("A", 4)
                    mm_group([(psA[a][:], gate[:, kc, j * 128:(j + 1) * 128], w_out_bf[:, kc, hc], kc == 0, kc == 7)
                              for kc in range(8)], gall + t_wout, [tA[a]])
                vop("dve", (lambda a, ob, hc: lambda e: e.tensor_tensor(ob[:, hc], psA[a][:], ob[:, hc], ALU.add))(a, ob, hc),
                    [tA[a], otk], [otk])
                if half == 0:
                    flush_pending()
            act(th[0][:].bitcast(BF16), ob, AF.Square, [otk], [tok("th0"), tok(f"ss2_{t}")], accum=ss2[:, t:t + 1])
            act(tmp2[:, t:t + 1], ss2[:, t:t + 1], AF.Ln, [tok(f"ss2_{t}"), tok("epsc")], [tok(f"tmp2_{t}")], scale=1.0 / D, bias=epsc[:, 0:1])
            act(rstd2[:, t:t + 1], tmp2[:, t:t + 1], AF.Exp, [tok(f"tmp2_{t}")], [tok(f"rstd2_{t}")], scale=-0.5)

            def fin_tile(t=t, ob=ob, otk=otk, rows=rows, j=j):
                vop("dve", lambda e: e.scalar_tensor_tensor(ob, ob, rstd2[:, t:t + 1], fing[:], ALU.mult, ALU.mult),
                    [otk, tok(f"rstd2_{t}"), tok("fing")], [otk])
                dma(out_d[rows, :], ob, [otk], [tok(f"outd{j}")], f"ostc{j}")
            if fin_q:
                fin_q.pop(0)()
            fin_q.append(fin_tile)
        while fin_q:
            fin_q.pop(0)()


    PRE_C = [0, 1, 2, 3, 8, 9, 10, 11, 12]
    POST_C = [4, 5, 6, 7, 13, 14, 15, 16]
    try:
        f0 = [front_tile_unit(0, j, 1) for j in range(4)]
        sec_inproj_pre(0)
        run_pipeline(f0[:2])
        stage_half0(range(0, 4))
        run_pipeline(f0[2:])
        stage_half0(range(4, 8))
        stage_half1_dmas()
        act(sinkexp[:], sinkraw[:], AF.Exp, [tok("sinkraw")], [tok("sinkexp")])
        for i in range(4):
            vop("dve", (lambda i: lambda e: e.tensor_scalar(sinkbc[:, i, :], neghalf[:, 0:128], 0.0,
                                                            sinkexp[:, i:i + 1], ALU.mult, ALU.add))(i),
                [tok("neghalf"), tok("sinkexp")], [tok("sinkbc")])
        kbanks = inproj_kouter([0, 1, 2, 3])
        for c in range(17):
            sec_inproj_chunks(0, [c], pre_banks=kbanks)
            if c < 8:
                cast_half1(c)
                if c < 4:
                    later_dma(c)
            elif c == 8:
                cast_small()
                dma(fing[:], fing_d[:, :], [], [tok("fing")], "fing")
        for k in range(8):
            sch.op("pool", (lambda k: lambda e: e.dma_start(out=w_out_bf[:, k, :], in_=w_out_d[k * 128:(k + 1) * 128, :]))(k),
                   reads=[], writes=[t_wout[k]], chan=f"wout{k}")
        for tb in range(nblocks):
            if tb > 0:
                sec_inproj_chunks(tb, POST_C)
            sec_inproj_post(tb)
            sec_stats(tb)
            sec_swa(tb)
            sec_upproj(tb)
            sec_mla(tb)
            if tb + 1 < nblocks:
                sec_inproj_pre(tb + 1)
                sec_inproj_chunks(tb + 1, PRE_C)
            sec_outproj(tb)
    except _Stop:
        pass

    if debug:
        dbuf = {"hT": (hT[:].rearrange("p a b -> p (a b)"), [tok("hT")]),
                "gate": (gate[:].rearrange("p a b -> p (a b)"), [tok(f"gate{i}") for i in range(8)]),
                "QnT": (QnT[:].rearrange("p a b -> p (a b)"), [tok("QnT")]),
                "QrT": (QrT[:].rearrange("p a b -> p (a b)"), [tok("QrT")]),
                "QsT": (QsT[:].rearrange("p a b -> p (a b)"), [tok("QsT")]),
                "KnT": (KnT[:].rearrange("p a b -> p (a b)"), [tok("KnT")]),
                "KrT": (KrT[:], [tok("KrT")]),
                "Vm": (Vm[:].rearrange("p a b -> p (a b)"), [tok("Vm")]),
                "cn": (cn_bf[:].rearrange("p a b -> p (a b)"), [tok("cn_bf0"), tok("cn_bf1"), tok("cn_bf2")]),
                "KsT": (KsT[:].rearrange("p a b -> p (a b)"), [tok("KsT")]),
                "Vs": (Vs[:].rearrange("p a b c -> p (a b c)"), [tok("Vs")]),
                "rstd4": (rstd4[:], [tok(f"rstd4_{t}") for t in range(NT)]),
                }
        dst = sb("dbgst", (128, 2048), F32)
        col = 0
        for (name, ncols) in debug["items"]:
            apf, rtoks = dbuf[name]
            for c1 in range(0, ncols, 2048):
                n = min(2048, ncols - c1)
                vop("dve", (lambda apf, c1, n: lambda e: e.tensor_copy(dst[:, 0:n], apf[:, c1:c1 + n]))(apf, c1, n),
                    rtoks, [tok("dbgst")])
                dma(dbg_d[:, col:col + n], dst[:, 0:n], [tok("dbgst")], [tok("dbgd")], "dbgc")
                col += n

    fin_reads = [tok(f"outd{i}") for i in range(4)] + ([tok("dbgd")] if debug else [])
    sch.op("sp", lambda e: e.nop(), reads=fin_reads, writes=[])

    sch.finalize()

    sems = {e: es.enter_context(nc.semaphore(f"sem_{e}")) for e in ENGS if e != "sp"}
    chan_sems = {c: es.enter_context(nc.semaphore(f"ch_{c}")) for c in sch.chan_cnt}
    with nc.Block() as block:
        @block.sync
        def _(e):
            sch.emit("sp", e, sems, chan_sems)

        @block.scalar
        def _(e):
            sch.emit("act", e, sems, chan_sems)

        @block.vector
        def _(e):
            sch.emit("dve", e, sems, chan_sems)

        @block.gpsimd
        def _(e):
            sch.emit("pool", e, sems, chan_sems)

        @block.tensor
        def _(e):
            sch.emit("pe", e, sems, chan_sems)
    es.close()
    return nc


def _host_consts():
    ident = np.eye(128, dtype=np.float32)
    rm = np.zeros((128, 128), np.float32)
    for do in range(128):
        if do % 64 < 32:
            rm[do + 32, do] = -1.0
        else:
            rm[do - 32, do] = 1.0
    k = np.arange(128)[:, None]
    q = np.arange(128)[None, :]
    maskc = np.where(k <= q, 0.0, NEG).astype(np.float32)
    maskp = np.where(k > q, 0.0, NEG).astype(np.float32)
    cst = np.zeros((128, 1024), np.float32)
    cst[:, 0:128] = ident
    cst[:, 128:256] = rm
    cst[:, 256:384] = maskc
    msw = np.concatenate([np.tile(maskc, (1, 4)), np.tile(maskp, (1, 4))], axis=1).astype(np.float32)
    pos = np.arange(S, dtype=np.float32)
    inv_freq = (1.0 / (np.float32(10000.0) ** (np.arange(0, 64, 2, dtype=np.float32) / np.float32(64)))).astype(np.float32)
    ang = (pos[None, :] * inv_freq[:, None]).astype(np.float32)
    cos32 = np.cos(ang).astype(np.float32)
    sin32 = np.sin(ang).astype(np.float32)
    cosT = np.tile(cos32, (4, 1)).astype(np.float32)
    sinT = np.tile(sin32, (4, 1)).astype(np.float32)
    return cst, msw, np.ascontiguousarray(cosT), np.ascontiguousarray(sinT)


def _layout_inputs(x, ln_mix, w_in, q_a_norm, w_q_up, kv_a_norm, w_kv_up, attn_sinks, w_out, final_norm):
    f = np.float32
    w_in = np.asarray(w_in, f)[0]
    cq = np.arange(0, 256)
    ckv = np.arange(256, 384)
    kr = np.arange(384, 448)
    gm = np.arange(448, 960)
    qs0 = 960
    ks = np.arange(1472, 1600)
    vs = np.arange(1600, 1728)
    gs0 = 1728

    def perm_heads(base):
        cols = []
        for i in range(4):
            cols.append(np.arange(base + i * 64, base + i * 64 + 64))
            cols.append(np.arange(base + (4 + i) * 64, base + (4 + i) * 64 + 64))
        return np.concatenate(cols)

    perm = np.concatenate([cq, ckv, kr, gm, perm_heads(qs0), ks, perm_heads(gs0), vs])
    w_in_p = np.ascontiguousarray(w_in[:, perm])
    wq = np.asarray(w_q_up, f)[0]
    qn = np.concatenate([np.arange(h * 192, h * 192 + 128) for h in range(4)])
    qr = np.concatenate([np.arange(h * 192 + 128, h * 192 + 192) for h in range(4)])
    wq_p = np.ascontiguousarray(wq[:, np.concatenate([qn, qr])])
    wkv = np.asarray(w_kv_up, f)[0]
    kn = np.concatenate([np.arange(h * 256, h * 256 + 128) for h in range(4)])
    vv = np.concatenate([np.arange(h * 256 + 128, h * 256 + 256) for h in range(4)])
    wkv_p = np.ascontiguousarray(wkv[:, np.concatenate([kn, vv])])
    wo = np.asarray(w_out, f)[0]
    rows = [np.arange(0, 512)]
    for i in range(4):
        for g in range(2):
            rows.append(np.arange(512 + (g * 4 + i) * 64, 512 + (g * 4 + i) * 64 + 64))
    wo_p = np.ascontiguousarray(wo[np.concatenate(rows), :])
    lng = np.ascontiguousarray(np.asarray(ln_mix, f)[0].reshape(8, 128).T)
    qag = np.ascontiguousarray(np.asarray(q_a_norm, f)[0].reshape(2, 128).T)
    kvag = np.ascontiguousarray(np.asarray(kv_a_norm, f)[0].reshape(1, 128).T)
    fing = np.ascontiguousarray(np.broadcast_to(np.asarray(final_norm, f)[None, :], (128, D)))
    sk = np.asarray(attn_sinks, f)[0]
    sinkbc = np.ascontiguousarray(np.concatenate([np.broadcast_to(sk[None, 0:4], (64, 4)),
                                                  np.broadcast_to(sk[None, 4:8], (64, 4))], axis=0))
    cst, msw, cosT, sinT = _host_consts()
    shared = {"w_in": w_in_p, "w_qup": wq_p, "w_kvup": wkv_p, "w_out": wo_p, "lng": lng, "qag": qag,
              "kvag": kvag, "fing": fing, "sinkbc": sinkbc, "cosT": cosT, "sinT": sinT, "cst": cst, "msw": msw}
    xs = np.asarray(x, f)
    return [dict(shared, x=np.ascontiguousarray(xs[b])) for b in range(8)]


_NC_CACHE = {}


def kernel(x, ln_mix, w_in, q_a_norm, w_q_up, kv_a_norm, w_kv_up, attn_sinks, w_out, final_norm):
    in_maps = _layout_inputs(x, ln_mix, w_in, q_a_norm, w_q_up, kv_a_norm, w_kv_up, attn_sinks, w_out, final_norm)
    nc = build_nc()
    res = run_bass_kernel_spmd(nc, in_maps, core_ids=list(range(8)))
    return np.stack([np.asarray(r["out"], np.float32) for r in res.results], axis=0)
```
